# Optimizing a Trainium2 kernel written in Bass

```python
import jax, jax.numpy as jnp
from jax import lax
import numpy as np

D_MODEL = 1024
BATCH = 8
SEQ = 2048
DEPTH = 1

HEAD_DIM = 64
N_HEADS_A = 8
N_KV_A = 2
N_HEADS_B = 8
N_KV_B = 2
WIDTH_A = N_HEADS_A * HEAD_DIM
WIDTH_B = N_HEADS_B * HEAD_DIM
MIX_WIDTH = WIDTH_A + WIDTH_B
KV_A = N_KV_A * HEAD_DIM
KV_B = N_KV_B * HEAD_DIM
SPLIT_SIZES = (WIDTH_A, KV_A, KV_A, WIDTH_A, WIDTH_B, KV_B, KV_B, WIDTH_B)
IN_COLS = sum(SPLIT_SIZES)
Q_BLOCK = 128
WINDOW = 128
GRID_W = 64
ROPE_THETA = 10000.0
N_BUCKETS = 32
MAX_DISTANCE = 128
EPS = 1e-6
MASK_VALUE = -1e30

kernel_name = "hymba_style_bidir_hybrid_attn_layer"


def rms_norm(x, g):
    xf = x.astype(jnp.float32)
    y = xf * lax.rsqrt(jnp.mean(xf * xf, axis=-1, keepdims=True) + EPS)
    return (y * g.astype(jnp.float32)).astype(x.dtype)


def _rotate(xs, ang):
    ang2 = jnp.concatenate([ang, ang], axis=-1)[None, :, None, :]
    cos, sin = jnp.cos(ang2), jnp.sin(ang2)
    x1, x2 = jnp.split(xs, 2, axis=-1)
    rot = jnp.concatenate([-x2, x1], axis=-1)
    return xs * cos + rot * sin


def axial_rope(x, row, col):
    half = x.shape[-1] // 2
    freqs = ROPE_THETA ** (-jnp.arange(0, half, 2, dtype=jnp.float32) / half)
    xf = x.astype(jnp.float32)
    xr = _rotate(xf[..., :half], row[:, None] * freqs[None, :])
    xc = _rotate(xf[..., half:], col[:, None] * freqs[None, :])
    return jnp.concatenate([xr, xc], axis=-1).astype(x.dtype)


def t5_bucket(rel):
    nb = N_BUCKETS // 2
    max_exact = nb // 2
    ret = (rel > 0).astype(jnp.int32) * nb
    n = jnp.abs(rel)
    nf = jnp.maximum(n, max_exact).astype(jnp.float32)
    large = max_exact + (jnp.log(nf / max_exact) / np.log(MAX_DISTANCE / max_exact)
                         * (nb - max_exact)).astype(jnp.int32)
    large = jnp.minimum(large, nb - 1)
    return ret + jnp.where(n < max_exact, n, large)


def global_attention(q, k, v):
    B, S, H, D = q.shape
    Hkv = k.shape[2]
    G = H // Hkv
    nb = S // Q_BLOCK
    scale = D ** -0.5
    qb = q.reshape(B, nb, Q_BLOCK, Hkv, G, D).transpose(1, 0, 2, 3, 4, 5)

    def one_block(qblk):
        s = jnp.einsum('bqkgd,bskd->bkgqs', qblk, k,
                       preferred_element_type=jnp.float32) * scale
        p = jax.nn.softmax(s, axis=-1)
        return jnp.einsum('bkgqs,bskd->bqkgd', p.astype(v.dtype), v)

    o = lax.map(one_block, qb)
    return o.transpose(1, 0, 2, 3, 4, 5).reshape(B, S, H * D)


def window_attention(q, k, v, rel_table, sink):
    B, S, H, D = q.shape
    Hkv = k.shape[2]
    G = H // Hkv
    nb = S // Q_BLOCK
    scale = D ** -0.5
    qb = q.reshape(B, nb, Q_BLOCK, Hkv, G, D)
    pad = ((0, 0), (WINDOW, WINDOW), (0, 0), (0, 0))
    kp = jnp.pad(k, pad).reshape(B, nb + 2, Q_BLOCK, Hkv, D)
    vp = jnp.pad(v, pad).reshape(B, nb + 2, Q_BLOCK, Hkv, D)
    kb = jnp.concatenate([kp[:, :-2], kp[:, 1:-1], kp[:, 2:]], axis=2)
    vb = jnp.concatenate([vp[:, :-2], vp[:, 1:-1], vp[:, 2:]], axis=2)
    r = jnp.arange(Q_BLOCK)
    j = jnp.arange(3 * Q_BLOCK)
    rel = j[None, :] - Q_BLOCK - r[:, None]
    bias = rel_table[t5_bucket(rel)]
    bias = bias.transpose(2, 0, 1).reshape(Hkv, G, Q_BLOCK, 3 * Q_BLOCK).astype(jnp.float32)
    kpos = jnp.arange(nb)[:, None] * Q_BLOCK - WINDOW + j[None, :]
    valid = ((jnp.abs(rel) <= WINDOW)[None]
             & (kpos >= 0)[:, None, :] & (kpos < S)[:, None, :])
    s = jnp.einsum('bnqkgd,bnskd->bnkgqs', qb, kb,
                   preferred_element_type=jnp.float32) * scale + bias
    s = jnp.where(valid[None, :, None, None], s, MASK_VALUE)
    sk = sink.astype(jnp.float32).reshape(Hkv, G)[None, None, :, :, None, None]
    m = jnp.maximum(jnp.max(s, axis=-1, keepdims=True), sk)
    e = jnp.exp(s - m)
    p = e / (jnp.sum(e, axis=-1, keepdims=True) + jnp.exp(sk - m))
    o = jnp.einsum('bnkgqs,bnskd->bnqkgd', p.astype(v.dtype), vb)
    return o.reshape(B, S, H * D)


def setup_inputs(seed: int = 0) -> dict:
    key = jax.random.key(seed)
    ks = jax.random.split(key, 14)
    f32 = jnp.float32
    x = jax.random.normal(ks[0], (BATCH, SEQ, D_MODEL), f32)
    c = jax.random.normal(ks[1], (BATCH, D_MODEL), f32)
    w_ada = jax.random.normal(ks[2], (DEPTH, D_MODEL, 3 * D_MODEL), f32) * (0.3 * D_MODEL ** -0.5)
    b_ada = jax.random.normal(ks[3], (DEPTH, 3 * D_MODEL), f32) * 0.01
    g_pre = 1.0 + 0.05 * jax.random.normal(ks[4], (DEPTH, D_MODEL), f32)
    g_post = 1.0 + 0.05 * jax.random.normal(ks[5], (DEPTH, D_MODEL), f32)
    w_in = jax.random.normal(ks[6], (DEPTH, D_MODEL, IN_COLS), f32) * D_MODEL ** -0.5
    qn_a = 1.0 + 0.05 * jax.random.normal(ks[7], (DEPTH, HEAD_DIM), f32)
    kn_a = 1.0 + 0.05 * jax.random.normal(ks[8], (DEPTH, HEAD_DIM), f32)
    sink_b = jax.random.normal(ks[9], (DEPTH, N_HEADS_B), f32)
    w_out = jax.random.normal(ks[10], (DEPTH, MIX_WIDTH, D_MODEL), f32) * MIX_WIDTH ** -0.5
    rel_table = jax.random.normal(ks[11], (N_BUCKETS, N_HEADS_B), f32) * 0.5
    return {"x": x, "c": c, "w_ada": w_ada, "b_ada": b_ada, "g_pre": g_pre,
            "g_post": g_post, "w_in": w_in, "qn_a": qn_a, "kn_a": kn_a,
            "sink_b": sink_b, "w_out": w_out, "rel_table": rel_table}


def reference(x, c, w_ada, b_ada, g_pre, g_post, w_in, qn_a, kn_a, sink_b, w_out, rel_table):
    B, S, _ = x.shape
    rows = S // GRID_W
    row = jnp.repeat(jnp.arange(rows), GRID_W).astype(jnp.float32)
    col = jnp.tile(jnp.arange(GRID_W), rows).astype(jnp.float32)
    split_idx = list(np.cumsum(SPLIT_SIZES)[:-1])
    c_act = jax.nn.silu(c)
    for l in range(DEPTH):
        mod = c_act @ w_ada[l] + b_ada[l]
        shift, scale, gate = jnp.split(mod, 3, axis=-1)
        h = rms_norm(x, g_pre[l]) * (1.0 + scale[:, None, :]) + shift[:, None, :]
        proj = h @ w_in[l]
        qa, ka, va, ga, qb, kb, vb, gb = jnp.split(proj, split_idx, axis=-1)
        qa = axial_rope(rms_norm(qa.reshape(B, S, N_HEADS_A, HEAD_DIM), qn_a[l]), row, col)
        ka = axial_rope(rms_norm(ka.reshape(B, S, N_KV_A, HEAD_DIM), kn_a[l]), row, col)
        va = va.reshape(B, S, N_KV_A, HEAD_DIM)
        oa = global_attention(qa, ka, va) * jax.nn.silu(ga)
        qb = qb.reshape(B, S, N_HEADS_B, HEAD_DIM)
        kb = kb.reshape(B, S, N_KV_B, HEAD_DIM)
        vb = vb.reshape(B, S, N_KV_B, HEAD_DIM)
        ob = window_attention(qb, kb, vb, rel_table, sink_b[l]) * jax.nn.silu(gb)
        y = jnp.concatenate([oa, ob], axis=-1) @ w_out[l]
        x = x + gate[:, None, :] * rms_norm(y, g_post[l])
    return x
```

```python
import contextlib
import numpy as np
import concourse.bass as bass
import concourse.mybir as mybir
from concourse.bass_utils import run_bass_kernel_spmd

F32 = mybir.dt.float32
BF16 = mybir.dt.bfloat16
AF = mybir.ActivationFunctionType
ALU = mybir.AluOpType
AX = mybir.AxisListType

S = 2048
D = 1024
NT = 16
KC = 8
EPS = 1e-6
N_CORES = 8


class Op:
    __slots__ = ("eng", "fn", "deps", "kind", "sem", "sem_val", "inc_needed", "inc_val", "idx")

    def __init__(self, eng, fn, kind):
        self.eng = eng
        self.fn = fn
        self.kind = kind
        self.deps = {}
        self.sem = None
        self.sem_val = 0
        self.inc_needed = False
        self.inc_val = 0


class Prog:
    ENGS = ("pe", "act", "dve", "pool", "sp")

    def __init__(self, nc, stack):
        self.nc = nc
        self.stack = stack
        self.ops = {e: [] for e in self.ENGS}
        self.fam = {}
        self.slots = {}
        self.eng_sem = {e: stack.enter_context(nc.semaphore("sem_" + e)) for e in ("pe", "act", "dve", "pool")}
        self.last = {}

    def _fam(self, r):
        return r if isinstance(r, str) else r[0]

    def _lookup(self, r):
        fam = self.fam.setdefault(self._fam(r), {"w": None, "r": [], "sub": {}})
        ws, rs = [], []
        if fam["w"] is not None:
            ws.append(fam["w"])
        rs.extend(fam["r"])
        if isinstance(r, str) or r[0] == "ps!":
            for sw, sr in fam["sub"].values():
                if sw is not None:
                    ws.append(sw)
                rs.extend(sr)
        else:
            sw, sr = fam["sub"].get(r, (None, []))
            if sw is not None:
                ws.append(sw)
            rs.extend(sr)
        return fam, ws, rs

    def _deps(self, op, reads, writes, extra):
        ps_reads = [r for r in reads if isinstance(r, tuple) and r[0] == "ps"]
        if ps_reads:
            reads = [r for r in reads if not (isinstance(r, tuple) and r[0] == "ps")]
            writes = list(writes) + ps_reads
        for r in reads:
            fam, ws, rs = self._lookup(r)
            for w in ws:
                if w is not op:
                    op.deps.setdefault(w, set()).add("raw")
        for wr in writes:
            fam, ws, rs = self._lookup(wr)
            for w in ws:
                if w is not op:
                    op.deps.setdefault(w, set()).add("waw")
            for rd in rs:
                if rd is not op:
                    op.deps.setdefault(rd, set()).add("war")
        for e in extra:
            if e is not None:
                op.deps.setdefault(e, set()).add("raw")
        for r in reads:
            fam = self.fam[self._fam(r)]
            if isinstance(r, str):
                fam["r"].append(op)
            else:
                ent = fam["sub"].setdefault(r, [None, []])
                ent[1].append(op)
        for wr in writes:
            fam = self.fam[self._fam(wr)]
            if isinstance(wr, str):
                fam["w"] = op
                fam["r"] = []
                fam["sub"] = {}
            else:
                fam["sub"][wr] = [op, []]

    tag = None

    def add(self, eng, fn, reads=(), writes=(), extra=()):
        if self.tag is not None and self.tag in KNOBS.get("skip", ()):
            return None
        op = Op(eng, fn, "c")
        self._deps(op, reads, writes, extra)
        self.ops[eng].append(op)
        self.last[eng] = op
        return op

    def dma(self, queue, slot, out_ap, in_ap, reads=(), writes=(), extra=()):
        if slot not in self.slots:
            self.slots[slot] = [self.stack.enter_context(self.nc.semaphore("dq_" + str(slot))), 0, None]
        st = self.slots[slot]
        op = Op(queue, lambda e: e.dma_start(out=out_ap, in_=in_ap), "dma")
        ex = list(extra)
        if st[2] is not None:
            ex.append(st[2])
        self._deps(op, reads, writes, ex)
        st[1] += 16
        op.sem = st[0]
        op.sem_val = st[1]
        st[2] = op
        self.ops[queue].append(op)
        return op

    def _needs_wait(self, d, kinds, eng):
        if d.kind == "dma":
            return True
        if d.eng != eng:
            return True
        if eng == "pe":
            return False
        return True

    def finalize(self):
        for e in self.ENGS:
            for op in self.ops[e]:
                for d, kinds in op.deps.items():
                    if d.kind == "c" and self._needs_wait(d, kinds, e):
                        d.inc_needed = True
        for e in self.ENGS:
            cnt = 0
            for op in self.ops[e]:
                if op.kind == "c" and op.inc_needed:
                    cnt += 1
                    op.inc_val = cnt

    def emit(self, eng_name, eng, final_waits=()):
        waited = {}
        for op in self.ops[eng_name]:
            for d, kinds in op.deps.items():
                if not self._needs_wait(d, kinds, eng_name):
                    continue
                if d.kind == "dma":
                    sem, val = d.sem, d.sem_val
                else:
                    sem, val = self.eng_sem[d.eng], d.inc_val
                if waited.get(sem.num, 0) >= val:
                    continue
                eng.wait_ge(sem, val)
                waited[sem.num] = val
            ins = op.fn(eng)
            if op.kind == "dma":
                ins.then_inc(op.sem, 16)
            elif op.inc_needed:
                ins.then_inc(self.eng_sem[eng_name], 1)
        for sem, val in final_waits:
            eng.wait_ge(sem, val)


def _t5_bucket(rel):
    nb = 16
    max_exact = 8
    ret = (rel > 0).astype(np.int32) * nb
    n = np.abs(rel)
    nf = np.maximum(n, max_exact).astype(np.float32)
    large = max_exact + (np.log(nf / np.float32(max_exact)) / np.float32(np.log(128 / max_exact)) * np.float32(nb - max_exact)).astype(np.int32)
    large = np.minimum(large, nb - 1)
    return ret + np.where(n < max_exact, n, large)


def _constants():
    t = np.arange(S)
    row = (t // 64).astype(np.float64)
    col = (t % 64).astype(np.float64)
    half = 32
    freqs = 10000.0 ** (-np.arange(0, half, 2, dtype=np.float64) / half)
    ang_r = row[:, None] * freqs[None, :]
    ang_c = col[:, None] * freqs[None, :]
    ang = np.concatenate([ang_r, ang_r, ang_c, ang_c], axis=-1)
    sign = np.where((np.arange(64) % 32) < 16, -1.0, 1.0)
    cos = np.cos(ang).astype(np.float32)
    ss = (np.sin(ang) * sign[None, :]).astype(np.float32)
    cosT = np.ascontiguousarray(cos.reshape(NT, 128, 64).transpose(1, 0, 2))
    ssT = np.ascontiguousarray(ss.reshape(NT, 128, 64).transpose(1, 0, 2))
    u = np.arange(512)
    rel = u - 255
    bucket = _t5_bucket(np.clip(rel, -255, 255))
    onehot = np.zeros((32, 512), np.float32)
    onehot[bucket, u] = 1.0
    mask = (np.abs(rel) <= 128).astype(np.float32)
    mask8 = np.ascontiguousarray(np.broadcast_to(mask[None, :], (8, 512)))
    ident = np.eye(128, dtype=np.float32)
    exch = np.ascontiguousarray(ident[::-1])
    return dict(cosT=cosT, ssT=ssT, onehot=onehot, mask8=mask8, ident=ident, exch=exch)


def build_program(stop=None):
    nc = bass.Bass("TRN2", target_bir_lowering=False)

    def din(name, shape):
        return nc.dram_tensor(name, list(shape), F32, kind="ExternalInput").ap()

    x_d = din("x", [S, D])
    cT_d = din("cT", [128, 8])
    wada_d = din("w_ada", [D, 3 * D])
    badaT_d = din("b_adaT", [128, 24])
    bgate_d = din("b_gate", [1, D])
    gpreT_d = din("g_preT", [128, 8])
    gpost_d = din("g_post", [1, D])
    win_d = din("w_in", [D, 2560])
    gains_d = din("gains", [1, 256])
    sink_d = din("sink", [1, 8])
    rel_d = din("rel_table", [32, 8])
    wout_d = din("w_out", [D, D])
    cosT_d = din("cosT", [128, NT, 64])
    ssT_d = din("ssT", [128, NT, 64])
    onehot_d = din("onehot", [32, 512])
    mask8_d = din("mask8", [8, 512])
    ident_d = din("ident", [128, 128])
    exch_d = din("exch", [128, 128])
    out_d = nc.dram_tensor("out", [S, D], F32, kind="ExternalOutput").ap()
    scr_t = nc.dram_tensor("scr", [8, 512], F32, kind="Internal")
    scr_d = scr_t.ap()

    dbg = {}

    with contextlib.ExitStack() as stack:
        TOTAL_WORDS = 52736
        big = stack.enter_context(nc.sbuf_tensor("big", [128, TOTAL_WORDS], F32))
        psum = stack.enter_context(nc.psum_tensor("psum", [128, 8, 512], F32))
        P = Prog(nc, stack)
        dumps = []

        def dump(name, ap, shape, dt):
            if "dumps" in KNOBS and name not in KNOBS["dumps"]:
                return
            d = nc.dram_tensor("dbg_" + name, list(shape), dt, kind="ExternalOutput").ap()
            P.dma("sp", ("dbg", name), d, ap, extra=[P.last.get(e) for e in ("pe", "act", "dve")])
            dumps.append(("dbg", name))

        def active(k):
            return stop is None or stop >= k

        def finish():
            P.finalize()
            final_waits = [(st[0], st[1]) for k, st in P.slots.items() if isinstance(k, tuple) and k[0] in ("out", "dbg")]
            with nc.Block() as block:
                @block.sync
                def _(e):
                    P.emit("sp", e, final_waits)

                @block.gpsimd
                def _(e):
                    P.emit("pool", e)

                @block.tensor
                def _(e):
                    P.emit("pe", e)

                @block.scalar
                def _(e):
                    P.emit("act", e)

                @block.vector
                def _(e):
                    P.emit("dve", e)


        off = [0]

        def region(words):
            o = off[0]
            off[0] += words
            return o

        o_hT = region(8192)
        o_G = region(8192)
        o_W = region(5632)
        o_QKA = region(6144)
        o_QB = region(4096)
        o_KB = region(2048)
        o_VA = region(1024)
        o_VB = region(1024)
        o_GG = region(1024)
        o_T = region(6144)
        o_T2 = region(3072)
        o_ET = region(3072)
        o_SM = region(1024)
        o_E = region(1024)
        assert off[0] <= TOTAL_WORDS, off[0]

        def f32v(o, n, parts=slice(0, 128)):
            return big[parts, o:o + n]

        def bf16v(o, nwords, parts=slice(0, 128)):
            return big[parts, o:o + nwords].bitcast(BF16)

        hT = bf16v(o_hT, 8192).rearrange("p (k t) -> p k t", k=8)
        GT = bf16v(o_G, 8192).rearrange("p (k t) -> p k t", k=8)
        Wtok = bf16v(o_W, 3584).rearrange("p (k n) -> p k n", k=8)
        wbuf = [bf16v(o_W + 3584 + i * 512, 512).rearrange("p (k n) -> p k n", k=8) for i in range(4)]
        Wout = bf16v(o_W, 4096).rearrange("p (k n) -> p k n", k=8)
        QKA = bf16v(o_QKA, 6144).rearrange("p (k t) -> p k t", k=6)
        QB = bf16v(o_QB, 4096).rearrange("p (k t) -> p k t", k=4)
        KB = bf16v(o_KB, 2048).rearrange("p (k t) -> p k t", k=2)
        VA = bf16v(o_VA, 1024).rearrange("p (b k d) -> p b k d", b=16, k=2)
        VB = bf16v(o_VB, 1024).rearrange("p (b k d) -> p b k d", b=16, k=2)
        GG = f32v(o_GG, 1024)
        ET = f32v(o_ET, 3072).rearrange("p (h c q) -> p h c q", h=8, c=3)

        sm = [o_SM]

        def small(words):
            o = sm[0]
            sm[0] += words
            return o

        ident_bf = bf16v(small(64), 64)
        ones_bf = bf16v(small(32), 32)
        c_sb = f32v(small(8), 8)
        e_sb = f32v(small(8), 8)
        cact_f = f32v(small(8), 8)
        cact_bf = bf16v(small(4), 4)
        modT = f32v(small(16), 16)
        a1 = f32v(small(8), 8)
        ss1 = f32v(small(16), 16)
        rstd1 = f32v(small(16), 16)
        ssq = f32v(small(160), 160)
        rq = f32v(small(160), 160)
        ss5 = f32v(small(16), 16)
        rstd5 = f32v(small(16), 16)
        esk8 = f32v(small(8), 8)
        ones_f = f32v(small(128), 128)
        badaT = f32v(small(24), 24)
        gpreT = f32v(small(8), 8)
        gains = f32v(small(256), 256)
        assert sm[0] <= o_SM + 1024

        def bank(b, parts=slice(0, 128)):
            return psum[parts, b, :]

        def bank_bf(b, parts=slice(0, 128)):
            return psum[parts, b, :].bitcast(BF16)

        P.dma("pool", "ident", ident_bf, ident_d, writes=["ident"])
        P.dma("sp", "c", c_sb, cT_d, writes=["c_sb"])
        P.dma("sp", "bada", badaT, badaT_d, writes=["badaT"])
        P.dma("sp", "gpre", gpreT, gpreT_d, writes=["gpreT"])
        P.add("dve", lambda e: e.memset(ones_bf, 1.0), writes=["ones_bf"])
        P.add("dve", lambda e: e.memset(ones_f, 1.0), writes=["ones_f"])
        P.add("dve", lambda e: e.memset(ss1, 0.0), writes=["ss1"])
        P.add("dve", lambda e: e.memset(ss5, 0.0), writes=["ss5"])

        o_rows = o_QB
        gate_row = big[0:1, o_rows:o_rows + 1024]
        bgate = big[0:1, o_rows + 1024:o_rows + 2048]
        gpost = big[0:1, o_rows + 2048:o_rows + 3072]
        ggrow = big[0:1, o_rows + 3072:o_rows + 4096]

        P.add("act", lambda e: e.activation(out=e_sb, in_=c_sb, func=AF.Exp, scale=-1.0), reads=["c_sb"], writes=["e_sb"])
        P.add("dve", lambda e: e.tensor_scalar(out=e_sb, in0=e_sb, scalar1=1.0, scalar2=1.0, op0=ALU.add, op1=ALU.mult),
              reads=["e_sb"], writes=["e_sb"])
        P.add("dve", lambda e: e.reciprocal(out=e_sb, in_=e_sb), reads=["e_sb"], writes=["e_sb"])
        P.add("dve", lambda e: e.tensor_tensor(out=cact_f, in0=c_sb, in1=e_sb, op=ALU.mult), reads=["c_sb", "e_sb"], writes=["cact_f"])
        P.add("dve", lambda e: e.tensor_copy(out=cact_bf, in_=cact_f), reads=["cact_f"], writes=["cact_bf"])

        cosS = f32v(o_T, 1024).rearrange("p (b d) -> p b d", b=16)
        ssS = f32v(o_T + 1024, 1024).rearrange("p (b d) -> p b d", b=16)
        CgQ = f32v(o_T + 2048, 1024).rearrange("p (b d) -> p b d", b=16)
        SgQ = f32v(o_T + 3072, 1024).rearrange("p (b d) -> p b d", b=16)
        CgK = f32v(o_T + 4096, 1024).rearrange("p (b d) -> p b d", b=16)
        SgK = f32v(o_T + 5120, 1024).rearrange("p (b d) -> p b d", b=16)

        def gbc(i):
            return gains[:, i * 64:(i + 1) * 64].unsqueeze(1).broadcast_to([128, 16, 64])

        def rope_tables():
            P.dma("sp", "cos", cosS, cosT_d, writes=["cosS"])
            P.dma("sp", "ss", ssS, ssT_d, writes=["ssS"])
            gains_src = bass.AP(gains_d.tensor, 0, [[0, 128], [1, 256]])
            P.dma("sp", "gains", gains, gains_src, writes=["gains"])
            P.add("dve", lambda e: e.scalar_tensor_tensor(out=CgQ, in0=cosS, scalar=0.125, in1=gbc(0), op0=ALU.mult, op1=ALU.mult),
                  reads=["cosS", "gains"], writes=["CgQ"])
            P.add("dve", lambda e: e.scalar_tensor_tensor(out=SgQ, in0=ssS, scalar=0.125, in1=gbc(1), op0=ALU.mult, op1=ALU.mult),
                  reads=["ssS", "gains"], writes=["SgQ"])
            P.add("dve", lambda e: e.tensor_tensor(out=CgK, in0=cosS, in1=gbc(2), op=ALU.mult), reads=["cosS", "gains"], writes=["CgK"])
            P.add("dve", lambda e: e.tensor_tensor(out=SgK, in0=ssS, in1=gbc(3), op=ALU.mult), reads=["ssS", "gains"], writes=["SgK"])

        rel33 = big[0:32, o_T2:o_T2 + 8]
        oh = big[0:32, o_T2 + 8:o_T2 + 520]
        frow = big[0:8, o_T2 + 520:o_T2 + 1032]
        m8 = big[0:8, o_T2 + 1032:o_T2 + 1544]
        exch_f = f32v(o_KB + 512, 128)
        hank = [f32v(o_KB + 640 + i * 385, 385) for i in range(2)]
        sinkrow = big[0:1, o_T2 + 2442:o_T2 + 2450]

        guard_ops = []

        def btab_head():
            P.dma("sp", "rel", rel33, rel_d, writes=["rel33"])
            P.dma("sp", "oh", oh, onehot_d, writes=["oh"])
            P.dma("sp", "m8", m8, mask8_d, writes=["m8"])
            P.dma("sp", "exch", exch_f, exch_d, writes=["exch"])
            P.dma("sp", "sink", sinkrow, sink_d, writes=["sinkrow"])
            P.add("pe", lambda e: e.matmul(bank(0, slice(0, 8)), rel33, oh, start=True, stop=True), reads=["rel33", "oh"], writes=[("ps", 0)])
            P.add("act", lambda e: e.activation(out=frow, in_=bank(0, slice(0, 8)), func=AF.Exp), reads=[("ps", 0)], writes=["frow"])
            P.add("dve", lambda e: e.tensor_tensor(out=frow, in0=frow, in1=m8, op=ALU.mult), reads=["frow", "m8"], writes=["frow"])
            guard_ops.append(P.dma("sp", "scr_w", scr_d, frow, reads=["frow"], writes=["scr"]))
            P.add("act", lambda e: e.activation(out=sinkrow, in_=sinkrow, func=AF.Exp), reads=["sinkrow"], writes=["sinkrow"])
            P.add("pe", lambda e: e.matmul(bank(3)[:, 0:8], ones_f[0:1, :], sinkrow, start=True, stop=True),
                  reads=["ones_f", "sinkrow"], writes=[("ps", 3)])
            P.add("dve", lambda e: e.tensor_copy(out=esk8, in_=bank(3)[:, 0:8]), reads=[("ps", 3)], writes=["esk8"])

        def btab_dma(h):
            hk_src = bass.AP(scr_t, h * 512, [[1, 128], [1, 385]])
            P.dma("sp", ("hank", h % 2), hank[h % 2], hk_src, reads=["scr"], writes=[("hank", h % 2)])

        def btab_compute(h):
            hk = hank[h % 2]
            rev = bass.AP(hk.tensor, hk.offset + 127, [[hk.ap[0][0], 128], [128, 3], [-1, 128]])
            P.add("dve", lambda e: e.tensor_copy(out=ET[:, h, :, :], in_=rev), reads=[("hank", h % 2)], writes=[("ET", h)])

        wada_v = wada_d.rearrange("(kc p) n -> p kc n", p=128)
        wab = [bf16v(o_G + i * 2048, 2048).rearrange("p (k n) -> p k n", k=8) for i in range(4)]
        MODB, GATEB0, GATEB1 = 5, 4, 3
        xt = [f32v(o_QB + i * 1024, 1024) for i in range(3)] + [f32v(o_QKA + i * 1024, 1024) for i in range(5)]
        NXT = len(xt)
        xn = [bf16v(o_QB + 3072 + i * 512, 512) for i in range(2)]
        junk = bf16v(o_KB, 512)
        x_v = x_d.rearrange("(b p) d -> b p d", p=128)
        out_v = out_d.rearrange("(b p) d -> b p d", p=128)
        TB0, TB1 = 6, 7
        ALL_HT = [("hT", t) for t in range(NT)]

        def wab_of(cb):
            return cb if cb < 4 else cb - 2

        def phase0_dma(cb):
            bi = wab_of(cb)
            P.dma("pool", ("wada", bi), wab[bi], wada_v[:, :, cb * 512:(cb + 1) * 512], writes=[("wada", bi)])

        def phase0_mm(cb, gbanks=(GATEB0, GATEB1)):
            bi = wab_of(cb)
            buf = wab[bi]
            if cb < 4:
                for j in range(4):
                    fc = cb * 4 + j
                    for kc in range(KC):
                        P.add("pe", lambda e, fc=fc, kc=kc, j=j: e.matmul(
                            bank(MODB)[:, fc:fc + 1], buf[:, kc, j * 128:(j + 1) * 128], cact_bf[:, kc:kc + 1],
                            start=(kc == 0), stop=(kc == KC - 1)),
                            reads=[("wada", bi), "cact_bf"], writes=[("ps", MODB)])
            else:
                gb = gbanks[cb - 4]
                for kc in range(KC):
                    P.add("pe", lambda e, kc=kc: e.matmul(
                        bank(gb, slice(0, 1)), cact_bf[:, kc:kc + 1], buf[:, kc, :],
                        start=(kc == 0), stop=(kc == KC - 1)),
                        reads=[("wada", bi), "cact_bf"], writes=[("ps", gb)])

        def phase1_front(tb):
            xi = tb % NXT
            P.dma("sp", ("xt", xi), xt[xi], x_v[tb], writes=[("xt", xi)])
            P.add("act", lambda e: e.activation(out=junk, in_=xt[xi], func=AF.Square, accum_out=ss1[:, tb:tb + 1]),
                  reads=[("xt", xi), "ss1"], writes=["junk", ("ss1", tb)])
            P.add("act", lambda e: e.activation(out=rstd1[:, tb:tb + 1], in_=ss1[:, tb:tb + 1], func=AF.Ln, scale=1.0 / D, bias=EPS_AP),
                  reads=[("ss1", tb), "eps"], writes=[("rstd1", tb)])
            P.add("act", lambda e: e.activation(out=rstd1[:, tb:tb + 1], in_=rstd1[:, tb:tb + 1], func=AF.Exp, scale=-0.5),
                  reads=[("rstd1", tb)], writes=[("rstd1", tb)])
            ni = tb % 2
            P.add("dve", lambda e: e.tensor_scalar_mul(out=xn[ni], in0=xt[xi], scalar1=rstd1[:, tb:tb + 1]),
                  reads=[("xt", xi), ("rstd1", tb)], writes=[("xn", ni)])
            tbank = TB0 if tb % 2 == 0 else TB1
            tps = bank_bf(tbank).rearrange("p (k t) -> p k t", k=8)
            for kc in range(KC):
                P.add("pe", lambda e, kc=kc: e.transpose(tps[:, kc, :], xn[ni][:, kc * 128:(kc + 1) * 128], ident_bf),
                      reads=[("xn", ni), "ident"], writes=[("ps", tbank)])

        def phase1_evac(tb):
            tbank = TB0 if tb % 2 == 0 else TB1
            tps = bank_bf(tbank).rearrange("p (k t) -> p k t", k=8)
            dst = hT[:, :, tb * 128:(tb + 1) * 128]
            P.add("dve", lambda e: e.tensor_copy(out=dst, in_=tps), reads=[("ps", tbank)], writes=[("hT", tb)])

        def phase1_block(tb):
            phase1_front(tb)
            if tb >= 1:
                phase1_evac(tb - 1)
            if tb == NT - 1:
                phase1_evac(tb)

        tb_groups = [[0, 1, 2, 3], [4, 5, 6, 7], [8, 9, 10, 11], [12, 13, 14, 15]]
        head_groups = [[], [0, 1, 2], [3, 4, 5], [6, 7]]
        for cb in range(4):
            phase0_dma(cb)
            phase0_mm(cb)
            for tb in tb_groups[cb]:
                phase1_block(tb)
        rope_tables()
        btab_head()
        btab_dma(0)
        win_v = win_d.rearrange("(kc p) n -> p kc n", p=128)
        P.dma("pool", "wtok0", Wtok[:, :, 0:512], win_v[:, :, 0:512], writes=[("Wtok", 0)])
        P.dma("pool", "wtok1", Wtok[:, :, 512:896], win_v[:, :, 512:896], writes=[("Wtok", 1)])
        phase0_dma(4)
        phase0_dma(5)
        ph1_last = [P.last["pe"], P.last["act"], P.last["dve"]]

        P.add("dve", lambda e: e.tensor_tensor(out=modT, in0=bank(MODB)[:, 0:16], in1=badaT[:, 0:16], op=ALU.add),
              reads=[("ps", MODB), "badaT"], writes=["modT"])
        P.add("dve", lambda e: e.scalar_tensor_tensor(out=a1, in0=modT[:, 8:16], scalar=1.0, in1=gpreT, op0=ALU.add, op1=ALU.mult),
              reads=["modT", "gpreT"], writes=["a1"])
        for kc in range(KC):
            if kc % 4 != 3:
                P.add("dve", lambda e, kc=kc: e.tensor_scalar(out=hT[:, kc, :], in0=hT[:, kc, :], scalar1=a1[:, kc:kc + 1],
                                                              scalar2=modT[:, kc:kc + 1], op0=ALU.mult, op1=ALU.add),
                      reads=ALL_HT + ["a1", "modT"], writes=[("hTm", kc)])
            else:
                P.add("act", lambda e, kc=kc: e.activation(out=hT[:, kc, :], in_=hT[:, kc, :], func=AF.Identity,
                                                           scale=a1[:, kc:kc + 1], bias=modT[:, kc:kc + 1]),
                      reads=ALL_HT + ["a1", "modT"], writes=[("hTm", kc)])
        HTM = [("hTm", kc) for kc in range(KC)]
        def gate_part():
            gbanks = (5, 6)
            P.dma("sp", "bgate", bgate, bgate_d, writes=["bgate"], extra=ph1_last)
            P.dma("sp", "gpost", gpost, gpost_d, writes=["gpost"], extra=ph1_last)
            phase0_mm(4, gbanks)
            phase0_mm(5, gbanks)
            for n, gb in ((0, gbanks[0]), (1, gbanks[1])):
                P.add("dve", lambda e, n=n, gb=gb: e.tensor_tensor(out=gate_row[:, n * 512:(n + 1) * 512], in0=bank(gb, slice(0, 1)),
                                                                  in1=bgate[:, n * 512:(n + 1) * 512], op=ALU.add),
                      reads=[("ps", gb), "bgate"], writes=[("gate_row", n)], extra=ph1_last)
                P.add("dve", lambda e, n=n: e.tensor_tensor(out=ggrow[:, n * 512:(n + 1) * 512], in0=gate_row[:, n * 512:(n + 1) * 512],
                                                           in1=gpost[:, n * 512:(n + 1) * 512], op=ALU.mult),
                      reads=[("gate_row", n), "gpost"], writes=[("ggrow", n)], extra=ph1_last)

        def gate_part_b():
            gbanks = (5, 6)
            for n, gb in ((0, gbanks[0]), (1, gbanks[1])):
                P.add("pe", lambda e, n=n, gb=gb: e.matmul(bank(gb), ones_f[0:1, :], ggrow[:, n * 512:(n + 1) * 512], start=True, stop=True),
                      reads=["ones_f", ("ggrow", n)], writes=[("ps", gb)])
                P.add("dve", lambda e, n=n, gb=gb: e.tensor_copy(out=GG[:, n * 512:(n + 1) * 512], in_=bank(gb)),
                      reads=[("ps", gb)], writes=[("GG", n)])

        ph05_last_dve = P.last["dve"]
        ph05_last_pe = P.last["pe"]

        if stop == 1:
            gate_part()
            gate_part_b()
            for h in range(8):
                if h + 1 < 8:
                    btab_dma(h + 1)
                btab_compute(h)
            dump("hT", hT, [128, 8, 2048], BF16)
            dump("modT", modT, [128, 16], F32)
            dump("GG", GG, [128, 1024], F32)
            dump("ET", ET, [128, 8, 3, 128], F32)
            dump("esk8", esk8, [128, 8], F32)
            dump("CgQ", CgQ, [128, 16, 64], F32)
            dump("SgK", SgK, [128, 16, 64], F32)
            finish()
            return nc
        P.add("dve", lambda e: e.memset(ssq, 0.0), writes=["ssq"])

        sq = f32v(o_T2, 640)
        t1 = f32v(o_T2 + 640, 640)
        t2 = sq
        qkf = f32v(o_T2 + 1280, 640)
        qk_tok = [bf16v(o_T2 + 1920 + i * 384, 384) for i in range(2)]
        t2_guard = [ph05_last_dve, ph05_last_pe, P.last["act"]] + guard_ops

        feat_jobs = [(m, nt) for m in range(13) for nt in range(4)]
        feat_state = {"i": 0}

        def emit_feat_job(banks=(4, 5, 6)):
            i = feat_state["i"]
            if i >= len(feat_jobs):
                return
            feat_state["i"] += 1
            m, nt = feat_jobs[i]
            wi = m % 4
            if nt == 0:
                P.dma("pool", ("wbuf", wi), wbuf[wi], win_v[:, :, 896 + m * 128:896 + (m + 1) * 128], writes=[("wbuf", wi)])
            fb = banks[i % len(banks)]
            for kc in range(KC):
                P.add("pe", lambda e, kc=kc, wi=wi, nt=nt, fb=fb: e.matmul(bank(fb), wbuf[wi][:, kc, :], hT[:, kc, nt * 512:(nt + 1) * 512],
                                                                            start=(kc == 0), stop=(kc == KC - 1)),
                      reads=[("wbuf", wi)] + HTM, writes=[("ps", fb)])
            tsl = slice(nt * 512, (nt + 1) * 512)
            if m < 4 or m >= 9:
                gch = m if m < 4 else m - 5
                etmp = f32v(o_hT_tmp + (i % 2) * 512, 512)
                P.add("act", lambda e, fb=fb, etmp=etmp: e.activation(out=etmp, in_=bank(fb), func=AF.Exp, scale=-1.0),
                      reads=[("ps", fb)], writes=[("etmp", i % 2)])
                P.add("act", lambda e, etmp=etmp: e.activation(out=etmp, in_=etmp, func=AF.Ln, scale=1.0, bias=1.0),
                      reads=[("etmp", i % 2)], writes=[("etmp", i % 2)])
                P.add("act", lambda e, etmp=etmp: e.activation(out=etmp, in_=etmp, func=AF.Exp, scale=-1.0),
                      reads=[("etmp", i % 2)], writes=[("etmp", i % 2)])
                P.add("dve", lambda e, fb=fb, etmp=etmp, gch=gch, tsl=tsl: e.tensor_tensor(out=GT[:, gch, tsl], in0=bank(fb), in1=etmp, op=ALU.mult),
                      reads=[("ps", fb), ("etmp", i % 2)], writes=[("GT", gch, nt)])
            elif m < 8:
                P.add("dve", lambda e, fb=fb, m=m, tsl=tsl: e.tensor_copy(out=QB[:, m - 4, tsl], in_=bank(fb)),
                      reads=[("ps", fb)], writes=[("QB", m - 4, nt)])
            else:
                P.add("act", lambda e, fb=fb, tsl=tsl: e.activation(out=KB[0:64, 0, tsl], in_=bank(fb, slice(0, 64)), func=AF.Copy),
                      reads=[("ps", fb)], writes=[("KBh", 0, nt), "junk", "exch", ("hank", 0), ("hank", 1)])
                P.add("act", lambda e, fb=fb, tsl=tsl: e.activation(out=KB[64:128, 1, tsl], in_=bank(fb, slice(64, 128)), func=AF.Copy),
                      reads=[("ps", fb)], writes=[("KBh", 1, nt), "junk", "exch", ("hank", 0), ("hank", 1)])
                P.dma("sp", ("kbd", 0), KB[64:128, 0, tsl], KB[0:64, 0, tsl], reads=[("KBh", 0, nt)], writes=[("KB", 0, nt), "junk", "exch", ("hank", 0), ("hank", 1)])
                P.dma("sp", ("kbd", 1), KB[0:64, 1, tsl], KB[64:128, 1, tsl], reads=[("KBh", 1, nt)], writes=[("KB", 1, nt), "junk", "exch", ("hank", 0), ("hank", 1)])

        o_hT_tmp = o_E

        def tok_mm(tb):
            tsl = slice(tb * 128, (tb + 1) * 128)
            b0, b1 = (0, 1) if tb % 2 == 0 else (2, 3)
            P.tag = 'tokmm'
            for n, bb in ((0, b0), (1, b1)):
                ncols = 512 if n == 0 else 384
                for kc in range(KC):
                    P.add("pe", lambda e, kc=kc, n=n, bb=bb, ncols=ncols, tsl=tsl: e.matmul(
                        bank(bb)[:, 0:ncols], hT[:, kc, tsl], Wtok[:, kc, n * 512:n * 512 + ncols],
                        start=(kc == 0), stop=(kc == KC - 1)),
                        reads=HTM + [("Wtok", n)], writes=[("ps", bb)])

        def tok_post(tb):
            tsl = slice(tb * 128, (tb + 1) * 128)
            b0, b1 = (0, 1) if tb % 2 == 0 else (2, 3)
            P.tag = 'v'
            P.add("act", lambda e, tb=tb, b1=b1: e.activation(out=VA[:, tb, :, :], in_=bank(b1)[:, 128:256].rearrange("p (k d) -> p k d", k=2), func=AF.Copy),
                  reads=[("ps", b1)], writes=[("VA", tb)])
            P.add("act", lambda e, tb=tb, b1=b1: e.activation(out=VB[:, tb, :, :], in_=bank(b1)[:, 256:384].rearrange("p (k d) -> p k d", k=2), func=AF.Copy),
                  reads=[("ps", b1)], writes=[("VB", tb)])
            P.tag = 'sq'
            P.add("act", lambda e, b0=b0: e.activation(out=sq[:, 0:512], in_=bank(b0), func=AF.Square),
                  reads=[("ps", b0)], writes=[("sq", "q")], extra=t2_guard)
            P.add("act", lambda e, b1=b1: e.activation(out=sq[:, 512:640], in_=bank(b1)[:, 0:128], func=AF.Square),
                  reads=[("ps", b1)], writes=[("sq", "k")], extra=t2_guard)
            P.add("act", lambda e, b0=b0: e.activation(out=qkf[:, 0:512], in_=bank(b0), func=AF.Copy),
                  reads=[("ps", b0)], writes=[("qkf", "q")], extra=t2_guard)
            P.add("act", lambda e, b1=b1: e.activation(out=qkf[:, 512:640], in_=bank(b1)[:, 0:128], func=AF.Copy),
                  reads=[("ps", b1)], writes=[("qkf", "k")], extra=t2_guard)
            P.tag = 'red'
            ssq_tb = ssq[:, tb * 10:(tb + 1) * 10]
            rq_tb = rq[:, tb * 10:(tb + 1) * 10]
            P.add("dve", lambda e, ssq_tb=ssq_tb: e.tensor_reduce(out=ssq_tb, in_=sq.rearrange("p (h d) -> p h d", d=64), axis=AX.X, op=ALU.add),
                  reads=["sq"], writes=[("ssq", tb)])
            P.add("act", lambda e, ssq_tb=ssq_tb, rq_tb=rq_tb: e.activation(out=rq_tb, in_=ssq_tb, func=AF.Ln, scale=1.0 / 64, bias=EPS_AP),
                  reads=[("ssq", tb), "eps"], writes=[("rq", tb)])
            P.add("act", lambda e, rq_tb=rq_tb: e.activation(out=rq_tb, in_=rq_tb, func=AF.Exp, scale=-0.5),
                  reads=[("rq", tb)], writes=[("rq", tb)])
            P.tag = 't1'
            P.add("dve", lambda e, b0=b0, tb=tb: e.tensor_tensor(out=t1[:, 0:512].rearrange("p (h d) -> p h d", d=64),
                                                                 in0=qkf[:, 0:512].rearrange("p (h d) -> p h d", d=64),
                                                                 in1=CgQ[:, tb, :].unsqueeze(1).broadcast_to([128, 8, 64]), op=ALU.mult),
                  reads=[("qkf", "q"), "CgQ"], writes=["t1q"], extra=t2_guard)
            P.add("dve", lambda e, b1=b1, tb=tb: e.tensor_tensor(out=t1[:, 512:640].rearrange("p (h d) -> p h d", d=64),
                                                                 in0=qkf[:, 512:640].rearrange("p (h d) -> p h d", d=64),
                                                                 in1=CgK[:, tb, :].unsqueeze(1).broadcast_to([128, 2, 64]), op=ALU.mult),
                  reads=[("qkf", "k"), "CgK"], writes=["t1k"], extra=t2_guard)
            P.tag = 't2'
            for half, (so, do) in enumerate(((16, 0), (0, 16))):
                P.add("dve", lambda e, b0=b0, tb=tb, so=so, do=do: e.tensor_tensor(
                    out=t2[:, 0:512].rearrange("p (h g w) -> p h g w", g=2, w=32)[:, :, :, do:do + 16],
                    in0=qkf[:, 0:512].rearrange("p (h g w) -> p h g w", g=2, w=32)[:, :, :, so:so + 16],
                    in1=SgQ[:, tb, :].rearrange("p (g w) -> p g w", w=32)[:, :, do:do + 16].unsqueeze(1).broadcast_to([128, 8, 2, 16]),
                    op=ALU.mult),
                    reads=[("qkf", "q"), "SgQ"], writes=[("sq", "t2q", half)], extra=t2_guard)
                P.add("dve", lambda e, b1=b1, tb=tb, so=so, do=do: e.tensor_tensor(
                    out=t2[:, 512:640].rearrange("p (h g w) -> p h g w", g=2, w=32)[:, :, :, do:do + 16],
                    in0=qkf[:, 512:640].rearrange("p (h g w) -> p h g w", g=2, w=32)[:, :, :, so:so + 16],
                    in1=SgK[:, tb, :].rearrange("p (g w) -> p g w", w=32)[:, :, do:do + 16].unsqueeze(1).broadcast_to([128, 2, 2, 16]),
                    op=ALU.mult),
                    reads=[("qkf", "k"), "SgK"], writes=[("sq", "t2k", half)], extra=t2_guard)
            P.tag = 'add'
            P.add("dve", lambda e: e.tensor_tensor(out=t1, in0=t1, in1=t2, op=ALU.add),
                  reads=["t1q", "t1k", "sq"], writes=["t1q", "t1k"])
            qi = tb % 2
            qkt = qk_tok[qi]
            P.tag = 'qkt'
            P.add("dve", lambda e, qkt=qkt, rq_tb=rq_tb: e.tensor_tensor(
                out=qkt[:, 0:512].rearrange("p (h d) -> p h d", d=64), in0=t1[:, 0:512].rearrange("p (h d) -> p h d", d=64),
                in1=rq_tb[:, 0:8].unsqueeze(2).broadcast_to([128, 8, 64]), op=ALU.mult),
                reads=["t1q", ("rq", tb)], writes=[("qkt%d" % qi, "q")])
            P.tag = 'dup'
            for kv in range(2):
                P.add("dve", lambda e, qkt=qkt, rq_tb=rq_tb, kv=kv: e.tensor_tensor(
                    out=qkt[:, 512 + kv * 128:512 + (kv + 1) * 128].rearrange("p (r d) -> p r d", d=64),
                    in0=t1[:, 512 + kv * 64:512 + (kv + 1) * 64].unsqueeze(1).broadcast_to([128, 2, 64]),
                    in1=rq_tb[:, 8 + kv:9 + kv].unsqueeze(2).broadcast_to([128, 2, 64]), op=ALU.mult),
                    reads=["t1k", ("rq", tb)], writes=[("qkt%d" % qi, "k", kv)])

        def tok_tr(tb):
            tsl = slice(tb * 128, (tb + 1) * 128)
            b0, b1 = (0, 1) if tb % 2 == 0 else (2, 3)
            qi = tb % 2
            qkt = qk_tok[qi]
            P.tag = 'tr'
            tbank = TB1
            tps = bank_bf(tbank).rearrange("p (k t) -> p k t", k=8)
            for c6 in range(6):
                P.add("pe", lambda e, c6=c6, qkt=qkt, tps=tps: e.transpose(tps[:, c6, :], qkt[:, c6 * 128:(c6 + 1) * 128], ident_bf),
                      reads=["qkt%d" % qi, "ident"], writes=[("ps", tbank)])
            P.add("act", lambda e, tps=tps, tsl=tsl: e.activation(out=QKA[:, :, tsl], in_=tps[:, 0:6, :], func=AF.Copy),
                  reads=[("ps", tbank)], writes=[("QKA", tb)])
            P.tag = None

        MERGE = KNOBS.get('merge', False)
        n_tb = KNOBS.get('p2_tb', NT)
        tok_mm(0)
        for tb in range(n_tb):
            if tb + 1 < n_tb:
                tok_mm(tb + 1)
            if tb < 8:
                if tb + 1 < 8:
                    btab_dma(tb + 1)
                btab_compute(tb)
            tok_post(tb)
            njobs = 1 if MERGE else (4 if tb < 4 else 3)
            for _ in range(njobs if KNOBS.get('p2_feat', True) else 0):
                emit_feat_job()
            tok_tr(tb)
            if tb == 1:
                gate_part()
            if tb == 3:
                gate_part_b()
        while feat_state["i"] < len(feat_jobs) and KNOBS.get('p2_feat', True) and (not MERGE or stop == 2):
            emit_feat_job()
        ph2_last_pe = P.last["pe"]
        ph2_last_dve = P.last["dve"]
        ph2_last_act = P.last["act"]


        if stop == 2:
            dump("QKA", QKA, [128, 6, 2048], BF16)
            dump("QB", QB, [128, 4, 2048], BF16)
            dump("KB", KB, [128, 2, 2048], BF16)
            dump("VA", bf16v(o_VA, 1024), [128, 2048], BF16)
            dump("VB", bf16v(o_VB, 1024), [128, 2048], BF16)
            dump("GT", GT, [128, 8, 2048], BF16)
            finish()
            return nc
        o_A = (o_T + 1024) if MERGE else o_hT
        PT = [bf16v(o_A + i * 512, 512).rearrange("p (u q) -> p u q", u=2) for i in range(3)]
        Rt = f32v(o_A + 1536, 512)
        Tt = f32v(o_A + 2048, 512)
        OBs = [f32v(o_A + 2560 + i * 512, 512) for i in range(2)]
        wout_v = wout_d.rearrange("(kc p) n -> p kc n", p=128)
        if not MERGE:
            P.dma("pool", "wout", Wout, wout_v, writes=["Wout"], extra=[ph2_last_pe])
        itersA = [(hp, qc, kb) for hp in range(4) for qc in range(4) for kb in range(NT)]

        def a_qk(i):
            hp, qc, kb = itersA[i]
            kv = hp // 2
            qsl = slice(qc * 512, (qc + 1) * 512)
            ksl = slice(kb * 128, (kb + 1) * 128)
            sL, sU = (0, 1) if i % 2 == 0 else (2, 3)
            rd = [("QKA", kb)] + [("QKA", t) for t in range(qc * 4, qc * 4 + 4)]
            P.add("pe", lambda e: e.matmul(bank(sL), QKA[0:64, 4 + kv, ksl], QKA[0:64, hp, qsl], start=True, stop=True),
                  reads=rd, writes=[("ps", sL)])
            P.add("pe", lambda e: e.matmul(bank(sU), QKA[64:128, 4 + kv, ksl], QKA[64:128, hp, qsl], start=True, stop=True),
                  reads=rd, writes=[("ps", sU)])

        def a_exp(i):
            sL, sU = (0, 1) if i % 2 == 0 else (2, 3)
            pi = i % 3
            P.add("act", lambda e: e.activation(out=PT[pi], in_=psum[:, sL:sL + 2, :], func=AF.Exp),
                  reads=[("ps", sL), ("ps", sU)], writes=[("PT", pi)])

        def a_pv(i):
            hp, qc, kb = itersA[i]
            kv = hp // 2
            g = i // NT
            OB1, OB2 = (4, 5) if (g % 2 == 0 or MERGE) else (6, 7)
            pi = i % 3
            st, sp_ = (kb == 0), (kb == NT - 1)
            P.add("pe", lambda e: e.matmul(bank(OB1, slice(0, 64)), VA[:, kb, kv, :], PT[pi][:, 0, :], start=st, stop=sp_),
                  reads=[("VA", kb), ("PT", pi)], writes=[("ps", OB1)])
            P.add("pe", lambda e: e.matmul(bank(OB1, slice(64, 128)), VA[:, kb, kv, :], PT[pi][:, 1, :], start=st, stop=sp_),
                  reads=[("VA", kb), ("PT", pi)], writes=[("ps", OB1)])
            P.add("pe", lambda e: e.matmul(bank(OB2, slice(0, 64)), ones_bf, PT[pi][:, 0, :], start=st, stop=sp_),
                  reads=["ones_bf", ("PT", pi)], writes=[("ps", OB2)])
            P.add("pe", lambda e: e.matmul(bank(OB2, slice(64, 128)), ones_bf, PT[pi][:, 1, :], start=st, stop=sp_),
                  reads=["ones_bf", ("PT", pi)], writes=[("ps", OB2)])

        def a_post(i):
            hp, qc, kb = itersA[i]
            g = i // NT
            OB1, OB2 = (4, 5) if (g % 2 == 0 or MERGE) else (6, 7)
            qsl = slice(qc * 512, (qc + 1) * 512)
            if MERGE:
                P.add("dve", lambda e: e.tensor_copy(out=OBs[0], in_=bank(OB1)), reads=[("ps", OB1)], writes=["OBs0"])
                P.add("dve", lambda e: e.tensor_copy(out=OBs[1], in_=bank(OB2)), reads=[("ps", OB2)], writes=["OBs1"])
                o_src, d_src = OBs[0], OBs[1]
                r1, r2 = ["OBs0"], ["OBs1"]
            else:
                o_src, d_src = bank(OB1), bank(OB2)
                r1, r2 = [("ps", OB1)], [("ps", OB2)]
            P.add("dve", lambda e: e.reciprocal(out=Rt, in_=d_src), reads=r2, writes=["RtL", "RtU"])
            P.add("dve", lambda e: e.tensor_tensor(out=Tt, in0=Rt, in1=GT[:, hp, qsl], op=ALU.mult),
                  reads=["RtL", "RtU"] + [("GT", hp, qc)], writes=["Tt"])
            P.add("dve", lambda e: e.tensor_tensor(out=GT[:, hp, qsl], in0=o_src, in1=Tt, op=ALU.mult),
                  reads=r1 + ["Tt"], writes=[("OG", hp, qc, 0), ("OG", hp, qc, 1)])

        nA = len(itersA) if KNOBS.get("p3", True) else 0
        if nA:
            a_qk(0)
            a_qk(1)
        for i in range(nA):
            a_exp(i)
            if i + 2 < nA:
                a_qk(i + 2)
            a_pv(i)
            if itersA[i][2] == NT - 1:
                a_post(i)
            if MERGE and i % 6 == 5:
                emit_feat_job(banks=(6, 7))
        if MERGE:
            while feat_state["i"] < len(feat_jobs):
                emit_feat_job(banks=(6, 7))
            P.dma("pool", "wout", Wout, wout_v, writes=["Wout"], extra=[P.last["pe"]])

        if stop == 3:
            dump("GT", GT, [128, 8, 2048], BF16)
            finish()
            return nc
        eskrow = big[0:1, o_E:o_E + 512].bitcast(BF16).rearrange("p (h q) -> p h q", h=8)
        P.add("dve", lambda e: e.tensor_copy(out=eskrow, in_=esk8[0:1, :].unsqueeze(2).broadcast_to([1, 8, 128])),
              reads=["esk8"], writes=["eskrow"], extra=[P.last["dve"], P.last["act"], P.last["pe"]])
        o_B = o_hT + 2560
        etmpB = [f32v(o_B + i * 1024, 1024).rearrange("p (u c) -> p u c", u=2) for i in range(2)]
        PTB = [bf16v(o_B + 2048 + i * 512, 512).rearrange("p (u c) -> p u c", u=2) for i in range(6)]
        RtB = Rt
        TtB = Tt

        def v4(ap):
            return ap.rearrange("p (b h q) -> p b h q", b=2, h=2)

        def bsl(ap, blk):
            return ap[:, blk * 256:(blk + 1) * 256]

        groupsB = ([(kv, np_) for np_ in range(NT // 2) for kv in range(2)] if KNOBS.get("overlap5", False)
                   else [(kv, np_) for kv in range(2) for np_ in range(NT // 2)])
        itersB = [(g, c) for g in range(len(groupsB)) for c in range(3)]

        def b_valid(n, c):
            return 0 <= n - 1 + c < NT

        def b_qk(i):
            g, c = itersB[i]
            kv, np_ = groupsB[g]
            sL, sU = (0, 1) if (i % 2 == 0 or KNOBS.get("overlap5", False)) else (2, 3)
            for blk in range(2):
                n = 2 * np_ + blk
                if not b_valid(n, c):
                    continue
                kb = n - 1 + c
                nsl = slice(n * 128, (n + 1) * 128)
                ksl = slice(kb * 128, (kb + 1) * 128)
                rd = [("KB", kv, kb // 4)] + [("QB", 2 * kv + j, n // 4) for j in range(2)]
                P.add("pe", lambda e, blk=blk, ksl=ksl, nsl=nsl: e.matmul(bsl(bank(sL), blk), KB[0:64, kv, ksl],
                                                                          QB[0:64, 2 * kv:2 * kv + 2, nsl], start=True, stop=True),
                      reads=rd, writes=[("ps", sL)])
                P.add("pe", lambda e, blk=blk, ksl=ksl, nsl=nsl: e.matmul(bsl(bank(sU), blk), KB[64:128, kv, ksl],
                                                                          QB[64:128, 2 * kv:2 * kv + 2, nsl], start=True, stop=True),
                      reads=rd, writes=[("ps", sU)])

        def b_exp(i):
            g, c = itersB[i]
            kv, np_ = groupsB[g]
            sL, sU = (0, 1) if (i % 2 == 0 or KNOBS.get("overlap5", False)) else (2, 3)
            ei = i % 2
            slot = (g % 2) * 3 + c
            P.add("act", lambda e: e.activation(out=etmpB[ei], in_=psum[:, sL:sL + 2, :], func=AF.Exp, scale=0.125),
                  reads=[("ps", sL), ("ps", sU)], writes=[("etmpB", ei)])
            for u in range(2):
                et_ap = ET[:, 4 * kv:4 * kv + 4, c, :].rearrange("p (hh u) q -> p u hh q", u=2)[:, u, :, :].unsqueeze(1).broadcast_to([128, 2, 2, 128])
                eng = KNOBS.get("b_mul_eng", "pool") if (u == 0 and c != 2) else "dve"
                P.add(eng, lambda e, u=u, et_ap=et_ap: e.tensor_tensor(out=v4(PTB[slot][:, u, :]), in0=v4(etmpB[ei][:, u, :]), in1=et_ap, op=ALU.mult),
                      reads=[("etmpB", ei)] + [("ET", 4 * kv + j) for j in range(4)], writes=[("PTB", slot, u)])

        def b_pv(g, part):
            kv, np_ = groupsB[g]
            OB1, OB2 = (4, 5) if g % 2 == 0 else (6, 7)
            for blk in range(2):
                n = 2 * np_ + blk
                cs = [c for c in range(3) if b_valid(n, c)]
                if blk == 0:
                    todo = [c for c in cs if (c <= 1) == (part == 0)]
                    open_groups = (part == 0)
                else:
                    todo = cs if part == 1 else []
                    open_groups = (part == 1)
                esk_u = eskrow[:, 4 * kv:4 * kv + 4, :].rearrange("p (hh u) q -> p u hh q", u=2)
                if open_groups:
                    P.add("pe", lambda e, blk=blk: e.matmul(bsl(bank(OB2, slice(64, 128)), blk), ones_bf[0:1, :], esk_u[:, 1, :, :],
                                                            start=True, stop=False),
                          reads=["ones_bf", "eskrow"], writes=[("ps", OB2)])
                    P.add("pe", lambda e, blk=blk: e.matmul(bsl(bank(OB2, slice(0, 64)), blk), ones_bf[0:1, :], esk_u[:, 0, :, :],
                                                            start=True, stop=False),
                          reads=["ones_bf", "eskrow"], writes=[("ps", OB2)])
                for c in todo:
                    kb = n - 1 + c
                    slot = (g % 2) * 3 + c
                    first, last = (c == cs[0]), (c == cs[-1])
                    ptl = bsl(PTB[slot][:, 0, :], blk)
                    ptu = bsl(PTB[slot][:, 1, :], blk)
                    rdp = [("PTB", slot, 0), ("PTB", slot, 1)]
                    P.add("pe", lambda e, blk=blk, kb=kb, ptl=ptl, first=first, last=last: e.matmul(
                        bsl(bank(OB1, slice(0, 64)), blk), VB[:, kb, kv, :], ptl, start=first, stop=last),
                        reads=[("VB", kb)] + rdp, writes=[("ps", OB1)])
                    P.add("pe", lambda e, blk=blk, kb=kb, ptu=ptu, first=first, last=last: e.matmul(
                        bsl(bank(OB1, slice(64, 128)), blk), VB[:, kb, kv, :], ptu, start=first, stop=last),
                        reads=[("VB", kb)] + rdp, writes=[("ps", OB1)])
                    P.add("pe", lambda e, blk=blk, ptl=ptl, first=first, last=last: e.matmul(
                        bsl(bank(OB2, slice(0, 64)), blk), ones_bf, ptl, start=False, stop=last),
                        reads=["ones_bf"] + rdp, writes=[("ps", OB2)])
                    P.add("pe", lambda e, blk=blk, ptu=ptu, first=first, last=last: e.matmul(
                        bsl(bank(OB2, slice(64, 128)), blk), ones_bf, ptu, start=False, stop=last),
                        reads=["ones_bf"] + rdp, writes=[("ps", OB2)])

        def b_post(g):
            kv, np_ = groupsB[g]
            OB1, OB2 = (4, 5) if g % 2 == 0 else (6, 7)
            tsl2 = slice(np_ * 256, (np_ + 1) * 256)
            P.add("act", lambda e: e.activation(out=RtB, in_=bank(OB2), func=AF.Ln), reads=[("ps", OB2)], writes=["RtL", "RtU"])
            P.add("act", lambda e: e.activation(out=RtB, in_=RtB, func=AF.Exp, scale=-1.0), reads=["RtL", "RtU"], writes=["RtL", "RtU"])

            def gview(parts):
                return GT[parts, 4 + 2 * kv:6 + 2 * kv, tsl2].rearrange("p h (b q) -> p b h q", b=2)
            gjobs = [("GT", 4 + 2 * kv + j, np_ // 2) for j in range(2)]
            P.add("dve", lambda e: e.tensor_tensor(out=v4(TtB), in0=v4(RtB), in1=gview(slice(0, 128)), op=ALU.mult),
                  reads=["RtL", "RtU"] + gjobs, writes=["Tt"])
            P.add("dve", lambda e: e.tensor_tensor(out=gview(slice(0, 128)), in0=v4(bank(OB1)), in1=v4(TtB), op=ALU.mult),
                  reads=[("ps", OB1), "Tt"], writes=[("OGB", kv, np_, 0), ("OGB", kv, np_, 1)])

        OVL = KNOBS.get("overlap5", False)
        ytile = [f32v(o_QKA + i * 1024, 1024) for i in range(3)]
        xr = [f32v(o_T + 1024 + i * 1024, 1024) for i in range(3)]
        junk5 = bf16v(o_T2, 512)
        p5_done = []

        def xr_load(tb):
            P.dma("sp", ("xr", tb % 3), xr[tb % 3], x_v[tb], writes=[("xr", tb % 3)], extra=[ph2_last_dve, ph2_last_act])

        def p5_block(tb, og_ready):
            if not p5_done:
                for t in range(3):
                    xr_load(t)
            p5_done.append(tb)
            tsl = slice(tb * 128, (tb + 1) * 128)
            if OVL:
                b0, b1 = 2, 3
            else:
                b0, b1 = (0, 1) if tb % 2 == 0 else (2, 3)
            xi = tb % 3
            for n, bb in ((0, b0), (1, b1)):
                for ch in range(8):
                    P.add("pe", lambda e, ch=ch, n=n, bb=bb: e.matmul(bank(bb), GT[:, ch, tsl], Wout[:, ch, n * 512:(n + 1) * 512],
                                                                      start=(ch == 0), stop=(ch == 7)),
                          reads=["Wout"], writes=[("ps", bb)], extra=og_ready)
            yps = psum[:, b0:b0 + 2, :]
            P.add("act", lambda e: e.activation(out=junk5.rearrange("p (u q) -> p u q", u=2), in_=yps, func=AF.Square,
                                                accum_out=ss5[:, tb:tb + 1]),
                  reads=[("ps", b0), ("ps", b1), "ss5"], writes=["junk5", ("ss5", tb)])
            P.add("act", lambda e: e.activation(out=rstd5[:, tb:tb + 1], in_=ss5[:, tb:tb + 1], func=AF.Ln, scale=1.0 / D, bias=EPS_AP),
                  reads=[("ss5", tb), "eps"], writes=[("rstd5", tb)])
            P.add("act", lambda e: e.activation(out=rstd5[:, tb:tb + 1], in_=rstd5[:, tb:tb + 1], func=AF.Exp, scale=-0.5),
                  reads=[("rstd5", tb)], writes=[("rstd5", tb)])
            yi = len(p5_done) % 3
            P.add("dve", lambda e: e.scalar_tensor_tensor(
                out=ytile[yi].rearrange("p (u q) -> p u q", u=2), in0=yps, scalar=rstd5[:, tb:tb + 1],
                in1=GG.rearrange("p (u q) -> p u q", u=2), op0=ALU.mult, op1=ALU.mult),
                reads=[("ps", b0), ("ps", b1), ("rstd5", tb), ("GG", 0), ("GG", 1)], writes=[("ytile", yi)])
            P.add("dve", lambda e: e.tensor_tensor(out=ytile[yi], in0=ytile[yi], in1=xr[xi], op=ALU.add),
                  reads=[("ytile", yi), ("xr", xi)], writes=[("ytile", yi)])
            P.dma("sp", ("out", yi), out_v[tb], ytile[yi], reads=[("ytile", yi)], writes=[("outd", tb)])
            if tb + 3 < NT:
                xr_load(tb + 3)

        nB = len(itersB)
        b_qk(0)
        if not OVL:
            b_qk(1)
        for i in range(nB):
            b_exp(i)
            if OVL:
                if i + 1 < nB:
                    b_qk(i + 1)
            elif i + 2 < nB:
                b_qk(i + 2)
            if itersB[i][1] == 1:
                b_pv(itersB[i][0], 0)
            if itersB[i][1] == 2:
                b_pv(itersB[i][0], 1)
                if itersB[i][0] >= 1:
                    gprev = itersB[i][0] - 1
                    b_post(gprev)
                    if OVL and KNOBS.get('p5_inloop', True) and groupsB[gprev][0] == 1:
                        rdy = [P.last["dve"]]
                        for t in (2 * groupsB[gprev][1], 2 * groupsB[gprev][1] + 1):
                            p5_block(t, rdy)
        b_post(len(groupsB) - 1)
        ph4_last_dve = P.last["dve"]
        ph4_last_act = P.last["act"]

        if stop == 4:
            dump("GT", GT, [128, 8, 2048], BF16)
            finish()
            return nc
        rdy_tail = [ph4_last_dve]
        for tb in range(NT):
            if tb not in p5_done:
                p5_block(tb, rdy_tail)

        finish()
    return nc


EPS_AP = EPS
KNOBS = {}


def _host_inputs(x, c, w_ada, b_ada, g_pre, g_post, w_in, qn_a, kn_a, sink_b, w_out, rel_table):
    f = np.float32
    x = np.asarray(x, f)
    c = np.asarray(c, f)
    w_ada = np.ascontiguousarray(np.asarray(w_ada, f)[0])
    b_ada = np.asarray(b_ada, f)[0]
    g_pre = np.asarray(g_pre, f)[0]
    g_post = np.asarray(g_post, f)[0]
    w_in = np.asarray(w_in, f)[0]
    qn = np.asarray(qn_a, f)[0]
    kn = np.asarray(kn_a, f)[0]
    sink = np.asarray(sink_b, f)[0]
    w_out = np.ascontiguousarray(np.asarray(w_out, f)[0])
    rel = np.ascontiguousarray(np.asarray(rel_table, f))
    qa, ka, va, ga = w_in[:, 0:512], w_in[:, 512:640], w_in[:, 640:768], w_in[:, 768:1280]
    qb, kb, vb, gb = w_in[:, 1280:1792], w_in[:, 1792:1920], w_in[:, 1920:2048], w_in[:, 2048:2560]
    w_in_r = np.ascontiguousarray(np.concatenate([qa, ka, va, vb, ga, qb, kb, gb], axis=1))
    assert w_in_r.shape == (1024, 2560)
    swap = np.array([i + 16 if (i % 32) < 16 else i - 16 for i in range(64)])
    gains = np.ascontiguousarray(np.concatenate([qn, qn[swap], kn, kn[swap]])[None, :])
    consts = _constants()
    shared = dict(
        w_ada=w_ada,
        b_adaT=np.ascontiguousarray(b_ada.reshape(24, 128).T),
        b_gate=np.ascontiguousarray(b_ada[2048:3072][None, :]),
        g_preT=np.ascontiguousarray(g_pre.reshape(8, 128).T),
        g_post=np.ascontiguousarray(g_post[None, :]),
        w_in=w_in_r,
        gains=gains,
        sink=np.ascontiguousarray(sink[None, :]),
        rel_table=rel,
        w_out=w_out,
        **consts,
    )
    in_maps = []
    for b in range(N_CORES):
        m = dict(shared)
        m["x"] = np.ascontiguousarray(x[b])
        m["cT"] = np.ascontiguousarray(c[b].reshape(8, 128).T)
        in_maps.append(m)
    return in_maps


_NC_CACHE = {}


def kernel(x, c, w_ada, b_ada, g_pre, g_post, w_in, qn_a, kn_a, sink_b, w_out, rel_table):
    in_maps = _host_inputs(x, c, w_ada, b_ada, g_pre, g_post, w_in, qn_a, kn_a, sink_b, w_out, rel_table)
    if "nc" not in _NC_CACHE:
        _NC_CACHE["nc"] = build_program()
    nc = _NC_CACHE["nc"]
    res = run_bass_kernel_spmd(nc, in_maps, core_ids=list(range(N_CORES)))
    out = np.stack([np.asarray(r["out"], np.float32) for r in res.results], axis=0)
    return out
```

```python
import contextlib
import numpy as np
import concourse.bass as bass
import concourse.mybir as mybir
from concourse.bass_utils import run_bass_kernel_spmd

F32 = mybir.dt.float32
BF16 = mybir.dt.bfloat16
AF = mybir.ActivationFunctionType
ALU = mybir.AluOpType
AX = mybir.AxisListType

S = 2048
D = 1024
NT = 16
KC = 8
EPS = 1e-6
N_CORES = 8


class Op:
    __slots__ = ("eng", "fn", "deps", "kind", "sem", "sem_val", "inc_needed", "inc_val", "idx")

    def __init__(self, eng, fn, kind):
        self.eng = eng
        self.fn = fn
        self.kind = kind
        self.deps = {}
        self.sem = None
        self.sem_val = 0
        self.inc_needed = False
        self.inc_val = 0


class Prog:
    ENGS = ("pe", "act", "dve", "pool", "sp")

    def __init__(self, nc, stack):
        self.nc = nc
        self.stack = stack
        self.ops = {e: [] for e in self.ENGS}
        self.fam = {}
        self.slots = {}
        self.eng_sem = {e: stack.enter_context(nc.semaphore("sem_" + e)) for e in ("pe", "act", "dve", "pool")}
        self.last = {}

    def _fam(self, r):
        return r if isinstance(r, str) else r[0]

    def _lookup(self, r):
        fam = self.fam.setdefault(self._fam(r), {"w": None, "r": [], "sub": {}})
        ws, rs = [], []
        if fam["w"] is not None:
            ws.append(fam["w"])
        rs.extend(fam["r"])
        if isinstance(r, str) or r[0] == "ps!":
            for sw, sr in fam["sub"].values():
                if sw is not None:
                    ws.append(sw)
                rs.extend(sr)
        else:
            sw, sr = fam["sub"].get(r, (None, []))
            if sw is not None:
                ws.append(sw)
            rs.extend(sr)
        return fam, ws, rs

    def _deps(self, op, reads, writes, extra):
        ps_reads = [r for r in reads if isinstance(r, tuple) and r[0] == "ps"]
        if ps_reads:
            reads = [r for r in reads if not (isinstance(r, tuple) and r[0] == "ps")]
            writes = list(writes) + ps_reads
        for r in reads:
            fam, ws, rs = self._lookup(r)
            for w in ws:
                if w is not op:
                    op.deps.setdefault(w, set()).add("raw")
        for wr in writes:
            fam, ws, rs = self._lookup(wr)
            for w in ws:
                if w is not op:
                    op.deps.setdefault(w, set()).add("waw")
            for rd in rs:
                if rd is not op:
                    op.deps.setdefault(rd, set()).add("war")
        for e in extra:
            if e is not None:
                op.deps.setdefault(e, set()).add("raw")
        for r in reads:
            fam = self.fam[self._fam(r)]
            if isinstance(r, str):
                fam["r"].append(op)
            else:
                ent = fam["sub"].setdefault(r, [None, []])
                ent[1].append(op)
        for wr in writes:
            fam = self.fam[self._fam(wr)]
            if isinstance(wr, str):
                fam["w"] = op
                fam["r"] = []
                fam["sub"] = {}
            else:
                fam["sub"][wr] = [op, []]

    tag = None

    def add(self, eng, fn, reads=(), writes=(), extra=()):
        if self.tag is not None and self.tag in KNOBS.get("skip", ()):
            return None
        op = Op(eng, fn, "c")
        self._deps(op, reads, writes, extra)
        self.ops[eng].append(op)
        self.last[eng] = op
        return op

    def dma(self, queue, slot, out_ap, in_ap, reads=(), writes=(), extra=()):
        if slot not in self.slots:
            self.slots[slot] = [self.stack.enter_context(self.nc.semaphore("dq_" + str(slot))), 0, None]
        st = self.slots[slot]
        op = Op(queue, lambda e: e.dma_start(out=out_ap, in_=in_ap), "dma")
        ex = list(extra)
        if st[2] is not None:
            ex.append(st[2])
        self._deps(op, reads, writes, ex)
        st[1] += 16
        op.sem = st[0]
        op.sem_val = st[1]
        st[2] = op
        self.ops[queue].append(op)
        return op

    def _needs_wait(self, d, kinds, eng):
        if d.kind == "dma":
            return True
        if d.eng != eng:
            return True
        if eng == "pe":
            return False
        return True

    def finalize(self):
        for e in self.ENGS:
            for op in self.ops[e]:
                for d, kinds in op.deps.items():
                    if d.kind == "c" and self._needs_wait(d, kinds, e):
                        d.inc_needed = True
        for e in self.ENGS:
            cnt = 0
            for op in self.ops[e]:
                if op.kind == "c" and op.inc_needed:
                    cnt += 1
                    op.inc_val = cnt

    def emit(self, eng_name, eng, final_waits=()):
        waited = {}
        for op in self.ops[eng_name]:
            for d, kinds in op.deps.items():
                if not self._needs_wait(d, kinds, eng_name):
                    continue
                if d.kind == "dma":
                    sem, val = d.sem, d.sem_val
                else:
                    sem, val = self.eng_sem[d.eng], d.inc_val
                if waited.get(sem.num, 0) >= val:
                    continue
                eng.wait_ge(sem, val)
                waited[sem.num] = val
            ins = op.fn(eng)
            if op.kind == "dma":
                ins.then_inc(op.sem, 16)
            elif op.inc_needed:
                ins.then_inc(self.eng_sem[eng_name], 1)
        for sem, val in final_waits:
            eng.wait_ge(sem, val)


def _t5_bucket(rel):
    nb = 16
    max_exact = 8
    ret = (rel > 0).astype(np.int32) * nb
    n = np.abs(rel)
    nf = np.maximum(n, max_exact).astype(np.float32)
    large = max_exact + (np.log(nf / np.float32(max_exact)) / np.float32(np.log(128 / max_exact)) * np.float32(nb - max_exact)).astype(np.int32)
    large = np.minimum(large, nb - 1)
    return ret + np.where(n < max_exact, n, large)


def _constants():
    t = np.arange(S)
    row = (t // 64).astype(np.float64)
    col = (t % 64).astype(np.float64)
    half = 32
    freqs = 10000.0 ** (-np.arange(0, half, 2, dtype=np.float64) / half)
    ang_r = row[:, None] * freqs[None, :]
    ang_c = col[:, None] * freqs[None, :]
    ang = np.concatenate([ang_r, ang_r, ang_c, ang_c], axis=-1)
    sign = np.where((np.arange(64) % 32) < 16, -1.0, 1.0)
    cos = np.cos(ang).astype(np.float32)
    ss = (np.sin(ang) * sign[None, :]).astype(np.float32)
    cosT = np.ascontiguousarray(cos.reshape(NT, 128, 64).transpose(1, 0, 2))
    ssT = np.ascontiguousarray(ss.reshape(NT, 128, 64).transpose(1, 0, 2))
    u = np.arange(512)
    rel = u - 255
    bucket = _t5_bucket(np.clip(rel, -255, 255))
    onehot = np.zeros((32, 512), np.float32)
    onehot[bucket, u] = 1.0
    mask = (np.abs(rel) <= 128).astype(np.float32)
    mask8 = np.ascontiguousarray(np.broadcast_to(mask[None, :], (8, 512)))
    ident = np.eye(128, dtype=np.float32)
    exch = np.ascontiguousarray(ident[::-1])
    return dict(cosT=cosT, ssT=ssT, onehot=onehot, mask8=mask8, ident=ident, exch=exch)


def build_program(stop=None):
    nc = bass.Bass("TRN2", target_bir_lowering=False)

    def din(name, shape):
        return nc.dram_tensor(name, list(shape), F32, kind="ExternalInput").ap()

    x_d = din("x", [S, D])
    cT_d = din("cT", [128, 8])
    wada_d = din("w_ada", [D, 3 * D])
    badaT_d = din("b_adaT", [128, 24])
    bgate_d = din("b_gate", [1, D])
    gpreT_d = din("g_preT", [128, 8])
    gpost_d = din("g_post", [1, D])
    win_d = din("w_in", [D, 2560])
    gains_d = din("gains", [1, 256])
    sink_d = din("sink", [1, 8])
    rel_d = din("rel_table", [32, 8])
    wout_d = din("w_out", [D, D])
    cosT_d = din("cosT", [128, NT, 64])
    ssT_d = din("ssT", [128, NT, 64])
    onehot_d = din("onehot", [32, 512])
    mask8_d = din("mask8", [8, 512])
    ident_d = din("ident", [128, 128])
    exch_d = din("exch", [128, 128])
    out_d = nc.dram_tensor("out", [S, D], F32, kind="ExternalOutput").ap()
    scr_t = nc.dram_tensor("scr", [8, 512], F32, kind="Internal")
    scr_d = scr_t.ap()

    dbg = {}

    with contextlib.ExitStack() as stack:
        TOTAL_WORDS = 52736
        big = stack.enter_context(nc.sbuf_tensor("big", [128, TOTAL_WORDS], F32))
        psum = stack.enter_context(nc.psum_tensor("psum", [128, 8, 512], F32))
        P = Prog(nc, stack)
        dumps = []

        def dump(name, ap, shape, dt):
            if "dumps" in KNOBS and name not in KNOBS["dumps"]:
                return
            d = nc.dram_tensor("dbg_" + name, list(shape), dt, kind="ExternalOutput").ap()
            P.dma("sp", ("dbg", name), d, ap, extra=[P.last.get(e) for e in ("pe", "act", "dve")])
            dumps.append(("dbg", name))

        def active(k):
            return stop is None or stop >= k

        def finish():
            P.finalize()
            final_waits = [(st[0], st[1]) for k, st in P.slots.items() if isinstance(k, tuple) and k[0] in ("out", "dbg")]
            with nc.Block() as block:
                @block.sync
                def _(e):
                    P.emit("sp", e, final_waits)

                @block.gpsimd
                def _(e):
                    P.emit("pool", e)

                @block.tensor
                def _(e):
                    P.emit("pe", e)

                @block.scalar
                def _(e):
                    P.emit("act", e)

                @block.vector
                def _(e):
                    P.emit("dve", e)


        off = [0]

        def region(words):
            o = off[0]
            off[0] += words
            return o

        o_hT = region(8192)
        o_G = region(8192)
        o_W = region(5632)
        o_QKA = region(6144)
        o_QB = region(4096)
        o_KB = region(2048)
        o_VA = region(1024)
        o_VB = region(1024)
        o_GG = region(1024)
        o_T = region(6144)
        o_T2 = region(3072)
        o_ET = region(3072)
        o_SM = region(1024)
        o_E = region(1024)
        assert off[0] <= TOTAL_WORDS, off[0]

        def f32v(o, n, parts=slice(0, 128)):
            return big[parts, o:o + n]

        def bf16v(o, nwords, parts=slice(0, 128)):
            return big[parts, o:o + nwords].bitcast(BF16)

        hT = bf16v(o_hT, 8192).rearrange("p (k t) -> p k t", k=8)
        GT = bf16v(o_G, 8192).rearrange("p (k t) -> p k t", k=8)
        Wtok = bf16v(o_W, 3584).rearrange("p (k n) -> p k n", k=8)
        wbuf = [bf16v(o_W + 3584 + i * 512, 512).rearrange("p (k n) -> p k n", k=8) for i in range(4)]
        Wout = bf16v(o_W, 4096).rearrange("p (k n) -> p k n", k=8)
        QKA = bf16v(o_QKA, 6144).rearrange("p (k t) -> p k t", k=6)
        QB = bf16v(o_QB, 4096).rearrange("p (k t) -> p k t", k=4)
        KB = bf16v(o_KB, 2048).rearrange("p (k t) -> p k t", k=2)
        VA = bf16v(o_VA, 1024).rearrange("p (b k d) -> p b k d", b=16, k=2)
        VB = bf16v(o_VB, 1024).rearrange("p (b k d) -> p b k d", b=16, k=2)
        GG = f32v(o_GG, 1024)
        ET = f32v(o_ET, 3072).rearrange("p (h c q) -> p h c q", h=8, c=3)

        sm = [o_SM]

        def small(words):
            o = sm[0]
            sm[0] += words
            return o

        ident_bf = bf16v(small(64), 64)
        ones_bf = bf16v(small(32), 32)
        c_sb = f32v(small(8), 8)
        e_sb = f32v(small(8), 8)
        cact_f = f32v(small(8), 8)
        cact_bf = bf16v(small(4), 4)
        modT = f32v(small(16), 16)
        a1 = f32v(small(8), 8)
        ss1 = f32v(small(16), 16)
        rstd1 = f32v(small(16), 16)
        ssq = f32v(small(160), 160)
        rq = f32v(small(160), 160)
        ss5 = f32v(small(16), 16)
        rstd5 = f32v(small(16), 16)
        esk8 = f32v(small(8), 8)
        ones_f = f32v(small(128), 128)
        badaT = f32v(small(24), 24)
        gpreT = f32v(small(8), 8)
        gains = f32v(small(256), 256)
        assert sm[0] <= o_SM + 1024

        def bank(b, parts=slice(0, 128)):
            return psum[parts, b, :]

        def bank_bf(b, parts=slice(0, 128)):
            return psum[parts, b, :].bitcast(BF16)

        P.dma("pool", "ident", ident_bf, ident_d, writes=["ident"])
        P.dma("sp", "c", c_sb, cT_d, writes=["c_sb"])
        P.dma("sp", "bada", badaT, badaT_d, writes=["badaT"])
        P.dma("sp", "gpre", gpreT, gpreT_d, writes=["gpreT"])
        P.add("dve", lambda e: e.memset(ones_bf, 1.0), writes=["ones_bf"])
        P.add("dve", lambda e: e.memset(ones_f, 1.0), writes=["ones_f"])
        P.add("dve", lambda e: e.memset(ss1, 0.0), writes=["ss1"])
        P.add("dve", lambda e: e.memset(ss5, 0.0), writes=["ss5"])

        o_rows = o_QB
        gate_row = big[0:1, o_rows:o_rows + 1024]
        bgate = big[0:1, o_rows + 1024:o_rows + 2048]
        gpost = big[0:1, o_rows + 2048:o_rows + 3072]
        ggrow = big[0:1, o_rows + 3072:o_rows + 4096]

        P.add("act", lambda e: e.activation(out=e_sb, in_=c_sb, func=AF.Exp, scale=-1.0), reads=["c_sb"], writes=["e_sb"])
        P.add("dve", lambda e: e.tensor_scalar(out=e_sb, in0=e_sb, scalar1=1.0, scalar2=1.0, op0=ALU.add, op1=ALU.mult),
              reads=["e_sb"], writes=["e_sb"])
        P.add("dve", lambda e: e.reciprocal(out=e_sb, in_=e_sb), reads=["e_sb"], writes=["e_sb"])
        P.add("dve", lambda e: e.tensor_tensor(out=cact_f, in0=c_sb, in1=e_sb, op=ALU.mult), reads=["c_sb", "e_sb"], writes=["cact_f"])
        P.add("dve", lambda e: e.tensor_copy(out=cact_bf, in_=cact_f), reads=["cact_f"], writes=["cact_bf"])

        cosS = f32v(o_T, 1024).rearrange("p (b d) -> p b d", b=16)
        ssS = f32v(o_T + 1024, 1024).rearrange("p (b d) -> p b d", b=16)
        CgQ = f32v(o_T + 2048, 1024).rearrange("p (b d) -> p b d", b=16)
        SgQ = f32v(o_T + 3072, 1024).rearrange("p (b d) -> p b d", b=16)
        CgK = f32v(o_T + 4096, 1024).rearrange("p (b d) -> p b d", b=16)
        SgK = f32v(o_T + 5120, 1024).rearrange("p (b d) -> p b d", b=16)

        def gbc(i):
            return gains[:, i * 64:(i + 1) * 64].unsqueeze(1).broadcast_to([128, 16, 64])

        def rope_tables():
            P.dma("sp", "cos", cosS, cosT_d, writes=["cosS"])
            P.dma("sp", "ss", ssS, ssT_d, writes=["ssS"])
            gains_src = bass.AP(gains_d.tensor, 0, [[0, 128], [1, 256]])
            P.dma("sp", "gains", gains, gains_src, writes=["gains"])
            P.add("dve", lambda e: e.scalar_tensor_tensor(out=CgQ, in0=cosS, scalar=0.125, in1=gbc(0), op0=ALU.mult, op1=ALU.mult),
                  reads=["cosS", "gains"], writes=["CgQ"])
            P.add("dve", lambda e: e.scalar_tensor_tensor(out=SgQ, in0=ssS, scalar=0.125, in1=gbc(1), op0=ALU.mult, op1=ALU.mult),
                  reads=["ssS", "gains"], writes=["SgQ"])
            P.add("dve", lambda e: e.tensor_tensor(out=CgK, in0=cosS, in1=gbc(2), op=ALU.mult), reads=["cosS", "gains"], writes=["CgK"])
            P.add("dve", lambda e: e.tensor_tensor(out=SgK, in0=ssS, in1=gbc(3), op=ALU.mult), reads=["ssS", "gains"], writes=["SgK"])

        rel33 = big[0:32, o_T2:o_T2 + 8]
        oh = big[0:32, o_T2 + 8:o_T2 + 520]
        frow = big[0:8, o_T2 + 520:o_T2 + 1032]
        m8 = big[0:8, o_T2 + 1032:o_T2 + 1544]
        exch_f = f32v(o_KB + 512, 128)
        hank = [f32v(o_KB + 640 + i * 385, 385) for i in range(2)]
        sinkrow = big[0:1, o_T2 + 2442:o_T2 + 2450]

        guard_ops = []

        def btab_head():
            P.dma("sp", "rel", rel33, rel_d, writes=["rel33"])
            P.dma("sp", "oh", oh, onehot_d, writes=["oh"])
            P.dma("sp", "m8", m8, mask8_d, writes=["m8"])
            P.dma("sp", "exch", exch_f, exch_d, writes=["exch"])
            P.dma("sp", "sink", sinkrow, sink_d, writes=["sinkrow"])
            P.add("pe", lambda e: e.matmul(bank(0, slice(0, 8)), rel33, oh, start=True, stop=True), reads=["rel33", "oh"], writes=[("ps", 0)])
            P.add("act", lambda e: e.activation(out=frow, in_=bank(0, slice(0, 8)), func=AF.Exp), reads=[("ps", 0)], writes=["frow"])
            P.add("dve", lambda e: e.tensor_tensor(out=frow, in0=frow, in1=m8, op=ALU.mult), reads=["frow", "m8"], writes=["frow"])
            guard_ops.append(P.dma("sp", "scr_w", scr_d, frow, reads=["frow"], writes=["scr"]))
            P.add("act", lambda e: e.activation(out=sinkrow, in_=sinkrow, func=AF.Exp), reads=["sinkrow"], writes=["sinkrow"])
            P.add("pe", lambda e: e.matmul(bank(3)[:, 0:8], ones_f[0:1, :], sinkrow, start=True, stop=True),
                  reads=["ones_f", "sinkrow"], writes=[("ps", 3)])
            P.add("dve", lambda e: e.tensor_copy(out=esk8, in_=bank(3)[:, 0:8]), reads=[("ps", 3)], writes=["esk8"])

        def btab_dma(h):
            hk_src = bass.AP(scr_t, h * 512, [[1, 128], [1, 385]])
            P.dma("sp", ("hank", h % 2), hank[h % 2], hk_src, reads=["scr"], writes=[("hank", h % 2)])

        def btab_compute(h):
            hk = hank[h % 2]
            rev = bass.AP(hk.tensor, hk.offset + 127, [[hk.ap[0][0], 128], [128, 3], [-1, 128]])
            P.add("dve", lambda e: e.tensor_copy(out=ET[:, h, :, :], in_=rev), reads=[("hank", h % 2)], writes=[("ET", h)])

        wada_v = wada_d.rearrange("(kc p) n -> p kc n", p=128)
        wab = [bf16v(o_G + i * 2048, 2048).rearrange("p (k n) -> p k n", k=8) for i in range(4)]
        MODB, GATEB0, GATEB1 = 5, 4, 3
        xt = [f32v(o_QB + i * 1024, 1024) for i in range(3)] + [f32v(o_QKA + i * 1024, 1024) for i in range(5)]
        NXT = len(xt)
        xn = [bf16v(o_QB + 3072 + i * 512, 512) for i in range(2)]
        junk = bf16v(o_KB, 512)
        x_v = x_d.rearrange("(b p) d -> b p d", p=128)
        out_v = out_d.rearrange("(b p) d -> b p d", p=128)
        TB0, TB1 = 6, 7
        ALL_HT = [("hT", t) for t in range(NT)]

        def wab_of(cb):
            return cb if cb < 4 else cb - 2

        def phase0_dma(cb):
            bi = wab_of(cb)
            P.dma("pool", ("wada", bi), wab[bi], wada_v[:, :, cb * 512:(cb + 1) * 512], writes=[("wada", bi)])

        def phase0_mm(cb, gbanks=(GATEB0, GATEB1)):
            bi = wab_of(cb)
            buf = wab[bi]
            if cb < 4:
                for j in range(4):
                    fc = cb * 4 + j
                    for kc in range(KC):
                        P.add("pe", lambda e, fc=fc, kc=kc, j=j: e.matmul(
                            bank(MODB)[:, fc:fc + 1], buf[:, kc, j * 128:(j + 1) * 128], cact_bf[:, kc:kc + 1],
                            start=(kc == 0), stop=(kc == KC - 1)),
                            reads=[("wada", bi), "cact_bf"], writes=[("ps", MODB)])
            else:
                gb = gbanks[cb - 4]
                for kc in range(KC):
                    P.add("pe", lambda e, kc=kc: e.matmul(
                        bank(gb, slice(0, 1)), cact_bf[:, kc:kc + 1], buf[:, kc, :],
                        start=(kc == 0), stop=(kc == KC - 1)),
                        reads=[("wada", bi), "cact_bf"], writes=[("ps", gb)])

        def phase1_front(tb):
            xi = tb % NXT
            P.dma("sp", ("xt", xi), xt[xi], x_v[tb], writes=[("xt", xi)])
            P.add("act", lambda e: e.activation(out=junk, in_=xt[xi], func=AF.Square, accum_out=ss1[:, tb:tb + 1]),
                  reads=[("xt", xi), "ss1"], writes=["junk", ("ss1", tb)])
            P.add("act", lambda e: e.activation(out=rstd1[:, tb:tb + 1], in_=ss1[:, tb:tb + 1], func=AF.Ln, scale=1.0 / D, bias=EPS_AP),
                  reads=[("ss1", tb), "eps"], writes=[("rstd1", tb)])
            P.add("act", lambda e: e.activation(out=rstd1[:, tb:tb + 1], in_=rstd1[:, tb:tb + 1], func=AF.Exp, scale=-0.5),
                  reads=[("rstd1", tb)], writes=[("rstd1", tb)])
            ni = tb % 2
            P.add("dve", lambda e: e.tensor_scalar_mul(out=xn[ni], in0=xt[xi], scalar1=rstd1[:, tb:tb + 1]),
                  reads=[("xt", xi), ("rstd1", tb)], writes=[("xn", ni)])
            tbank = TB0 if tb % 2 == 0 else TB1
            tps = bank_bf(tbank).rearrange("p (k t) -> p k t", k=8)
            for kc in range(KC):
                P.add("pe", lambda e, kc=kc: e.transpose(tps[:, kc, :], xn[ni][:, kc * 128:(kc + 1) * 128], ident_bf),
                      reads=[("xn", ni), "ident"], writes=[("ps", tbank)])

        def phase1_evac(tb):
            tbank = TB0 if tb % 2 == 0 else TB1
            tps = bank_bf(tbank).rearrange("p (k t) -> p k t", k=8)
            dst = hT[:, :, tb * 128:(tb + 1) * 128]
            P.add("dve", lambda e: e.tensor_copy(out=dst, in_=tps), reads=[("ps", tbank)], writes=[("hT", tb)])

        def phase1_block(tb):
            phase1_front(tb)
            if tb >= 1:
                phase1_evac(tb - 1)
            if tb == NT - 1:
                phase1_evac(tb)

        tb_groups = [[0, 1, 2, 3], [4, 5, 6, 7], [8, 9, 10, 11], [12, 13, 14, 15]]
        head_groups = [[], [0, 1, 2], [3, 4, 5], [6, 7]]
        for cb in range(4):
            phase0_dma(cb)
            phase0_mm(cb)
            for tb in tb_groups[cb]:
                phase1_block(tb)
        rope_tables()
        btab_head()
        btab_dma(0)
        win_v = win_d.rearrange("(kc p) n -> p kc n", p=128)
        P.dma("pool", "wtok0", Wtok[:, :, 0:512], win_v[:, :, 0:512], writes=[("Wtok", 0)])
        P.dma("pool", "wtok1", Wtok[:, :, 512:896], win_v[:, :, 512:896], writes=[("Wtok", 1)])
        phase0_dma(4)
        phase0_dma(5)
        ph1_last = [P.last["pe"], P.last["act"], P.last["dve"]]

        P.add("dve", lambda e: e.tensor_tensor(out=modT, in0=bank(MODB)[:, 0:16], in1=badaT[:, 0:16], op=ALU.add),
              reads=[("ps", MODB), "badaT"], writes=["modT"])
        P.add("dve", lambda e: e.scalar_tensor_tensor(out=a1, in0=modT[:, 8:16], scalar=1.0, in1=gpreT, op0=ALU.add, op1=ALU.mult),
              reads=["modT", "gpreT"], writes=["a1"])
        for kc in range(KC):
            if kc % 4 != 3:
                P.add("dve", lambda e, kc=kc: e.tensor_scalar(out=hT[:, kc, :], in0=hT[:, kc, :], scalar1=a1[:, kc:kc + 1],
                                                              scalar2=modT[:, kc:kc + 1], op0=ALU.mult, op1=ALU.add),
                      reads=ALL_HT + ["a1", "modT"], writes=[("hTm", kc)])
            else:
                P.add("act", lambda e, kc=kc: e.activation(out=hT[:, kc, :], in_=hT[:, kc, :], func=AF.Identity,
                                                           scale=a1[:, kc:kc + 1], bias=modT[:, kc:kc + 1]),
                      reads=ALL_HT + ["a1", "modT"], writes=[("hTm", kc)])
        HTM = [("hTm", kc) for kc in range(KC)]
        def gate_part():
            gbanks = (5, 6)
            P.dma("sp", "bgate", bgate, bgate_d, writes=["bgate"], extra=ph1_last)
            P.dma("sp", "gpost", gpost, gpost_d, writes=["gpost"], extra=ph1_last)
            phase0_mm(4, gbanks)
            phase0_mm(5, gbanks)
            for n, gb in ((0, gbanks[0]), (1, gbanks[1])):
                P.add("dve", lambda e, n=n, gb=gb: e.tensor_tensor(out=gate_row[:, n * 512:(n + 1) * 512], in0=bank(gb, slice(0, 1)),
                                                                  in1=bgate[:, n * 512:(n + 1) * 512], op=ALU.add),
                      reads=[("ps", gb), "bgate"], writes=[("gate_row", n)], extra=ph1_last)
                P.add("dve", lambda e, n=n: e.tensor_tensor(out=ggrow[:, n * 512:(n + 1) * 512], in0=gate_row[:, n * 512:(n + 1) * 512],
                                                           in1=gpost[:, n * 512:(n + 1) * 512], op=ALU.mult),
                      reads=[("gate_row", n), "gpost"], writes=[("ggrow", n)], extra=ph1_last)

        def gate_part_b():
            gbanks = (5, 6)
            for n, gb in ((0, gbanks[0]), (1, gbanks[1])):
                P.add("pe", lambda e, n=n, gb=gb: e.matmul(bank(gb), ones_f[0:1, :], ggrow[:, n * 512:(n + 1) * 512], start=True, stop=True),
                      reads=["ones_f", ("ggrow", n)], writes=[("ps", gb)])
                P.add("dve", lambda e, n=n, gb=gb: e.tensor_copy(out=GG[:, n * 512:(n + 1) * 512], in_=bank(gb)),
                      reads=[("ps", gb)], writes=[("GG", n)])

        ph05_last_dve = P.last["dve"]
        ph05_last_pe = P.last["pe"]

        if stop == 1:
            gate_part()
            gate_part_b()
            for h in range(8):
                if h + 1 < 8:
                    btab_dma(h + 1)
                btab_compute(h)
            dump("hT", hT, [128, 8, 2048], BF16)
            dump("modT", modT, [128, 16], F32)
            dump("GG", GG, [128, 1024], F32)
            dump("ET", ET, [128, 8, 3, 128], F32)
            dump("esk8", esk8, [128, 8], F32)
            dump("CgQ", CgQ, [128, 16, 64], F32)
            dump("SgK", SgK, [128, 16, 64], F32)
            finish()
            return nc
        P.add("dve", lambda e: e.memset(ssq, 0.0), writes=["ssq"])

        sq = f32v(o_T2, 640)
        t1 = f32v(o_T2 + 640, 640)
        t2 = sq
        qkf = f32v(o_T2 + 1280, 640)
        qk_tok = [bf16v(o_T2 + 1920 + i * 384, 384) for i in range(2)]
        t2_guard = [ph05_last_dve, ph05_last_pe, P.last["act"]] + guard_ops

        ja = [(m, nt) for m in (0, 1, 2, 3) for nt in range(4)]
        jg = [(m, nt) for m in (9, 10, 11, 12) for nt in range(4)]
        jl = [(m, nt) for m in (4, 5, 6, 7, 8) for nt in range(4)]
        feat_jobs = list(ja)
        ig = il = 0
        for blk in range(12):
            take = ["g", "l", "g"] if blk % 3 == 0 else ["l", "g", "l"]
            for t in take:
                if t == "g" and ig < len(jg):
                    feat_jobs.append(jg[ig]); ig += 1
                elif il < len(jl):
                    feat_jobs.append(jl[il]); il += 1
                elif ig < len(jg):
                    feat_jobs.append(jg[ig]); ig += 1
        feat_jobs += jg[ig:] + jl[il:]
        assert len(feat_jobs) == 52 and len(set(feat_jobs)) == 52
        wslot = {0: 0, 1: 1, 2: 0, 3: 1, 9: 0, 10: 1, 11: 0, 12: 1, 4: 2, 5: 3, 6: 2, 7: 3, 8: 2}
        feat_state = {"i": 0}

        def emit_feat_job(banks=(4, 5, 6)):
            i = feat_state["i"]
            if i >= len(feat_jobs):
                return
            feat_state["i"] += 1
            m, nt = feat_jobs[i]
            wi = wslot[m]
            if nt == 0:
                P.dma("pool", ("wbuf", wi), wbuf[wi], win_v[:, :, 896 + m * 128:896 + (m + 1) * 128], writes=[("wbuf", wi)])
            fb = banks[i % len(banks)]
            for kc in range(KC):
                P.add("pe", lambda e, kc=kc, wi=wi, nt=nt, fb=fb: e.matmul(bank(fb), wbuf[wi][:, kc, :], hT[:, kc, nt * 512:(nt + 1) * 512],
                                                                            start=(kc == 0), stop=(kc == KC - 1)),
                      reads=[("wbuf", wi)] + HTM, writes=[("ps", fb)])
            tsl = slice(nt * 512, (nt + 1) * 512)
            if m < 4 or m >= 9:
                gch = m if m < 4 else m - 5
                etmp = f32v(o_hT_tmp + (i % 2) * 512, 512)
                P.add("act", lambda e, fb=fb, etmp=etmp: e.activation(out=etmp, in_=bank(fb), func=AF.Exp, scale=-1.0),
                      reads=[("ps", fb)], writes=[("etmp", i % 2)])
                P.add("act", lambda e, etmp=etmp: e.activation(out=etmp, in_=etmp, func=AF.Ln, scale=1.0, bias=1.0),
                      reads=[("etmp", i % 2)], writes=[("etmp", i % 2)])
                P.add("act", lambda e, etmp=etmp: e.activation(out=etmp, in_=etmp, func=AF.Exp, scale=-1.0),
                      reads=[("etmp", i % 2)], writes=[("etmp", i % 2)])
                P.add("dve", lambda e, fb=fb, etmp=etmp, gch=gch, tsl=tsl: e.tensor_tensor(out=GT[:, gch, tsl], in0=bank(fb), in1=etmp, op=ALU.mult),
                      reads=[("ps", fb), ("etmp", i % 2)], writes=[("GT", gch, nt)])
            elif m < 8:
                P.add("dve", lambda e, fb=fb, m=m, tsl=tsl: e.tensor_copy(out=QB[:, m - 4, tsl], in_=bank(fb)),
                      reads=[("ps", fb)], writes=[("QB", m - 4, nt)])
            else:
                P.add("act", lambda e, fb=fb, tsl=tsl: e.activation(out=KB[0:64, 0, tsl], in_=bank(fb, slice(0, 64)), func=AF.Copy),
                      reads=[("ps", fb)], writes=[("KBh", 0, nt), "junk", "exch", ("hank", 0), ("hank", 1)])
                P.add("act", lambda e, fb=fb, tsl=tsl: e.activation(out=KB[64:128, 1, tsl], in_=bank(fb, slice(64, 128)), func=AF.Copy),
                      reads=[("ps", fb)], writes=[("KBh", 1, nt), "junk", "exch", ("hank", 0), ("hank", 1)])
                P.dma("sp", ("kbd", 0), KB[64:128, 0, tsl], KB[0:64, 0, tsl], reads=[("KBh", 0, nt)], writes=[("KB", 0, nt), "junk", "exch", ("hank", 0), ("hank", 1)])
                P.dma("sp", ("kbd", 1), KB[0:64, 1, tsl], KB[64:128, 1, tsl], reads=[("KBh", 1, nt)], writes=[("KB", 1, nt), "junk", "exch", ("hank", 0), ("hank", 1)])

        o_hT_tmp = o_E

        def tok_mm(tb):
            tsl = slice(tb * 128, (tb + 1) * 128)
            b0, b1 = (0, 1) if tb % 2 == 0 else (2, 3)
            P.tag = 'tokmm'
            for n, bb in ((0, b0), (1, b1)):
                ncols = 512 if n == 0 else 384
                for kc in range(KC):
                    P.add("pe", lambda e, kc=kc, n=n, bb=bb, ncols=ncols, tsl=tsl: e.matmul(
                        bank(bb)[:, 0:ncols], hT[:, kc, tsl], Wtok[:, kc, n * 512:n * 512 + ncols],
                        start=(kc == 0), stop=(kc == KC - 1)),
                        reads=HTM + [("Wtok", n)], writes=[("ps", bb)])

        def tok_post(tb):
            tsl = slice(tb * 128, (tb + 1) * 128)
            b0, b1 = (0, 1) if tb % 2 == 0 else (2, 3)
            P.tag = 'v'
            P.add("act", lambda e, tb=tb, b1=b1: e.activation(out=VA[:, tb, :, :], in_=bank(b1)[:, 128:256].rearrange("p (k d) -> p k d", k=2), func=AF.Copy),
                  reads=[("ps", b1)], writes=[("VA", tb)])
            P.add("act", lambda e, tb=tb, b1=b1: e.activation(out=VB[:, tb, :, :], in_=bank(b1)[:, 256:384].rearrange("p (k d) -> p k d", k=2), func=AF.Copy),
                  reads=[("ps", b1)], writes=[("VB", tb)])
            P.tag = 'sq'
            P.add("act", lambda e, b0=b0: e.activation(out=sq[:, 0:512], in_=bank(b0), func=AF.Square),
                  reads=[("ps", b0)], writes=[("sq", "q")], extra=t2_guard)
            P.add("act", lambda e, b1=b1: e.activation(out=sq[:, 512:640], in_=bank(b1)[:, 0:128], func=AF.Square),
                  reads=[("ps", b1)], writes=[("sq", "k")], extra=t2_guard)
            P.add("act", lambda e, b0=b0: e.activation(out=qkf[:, 0:512], in_=bank(b0), func=AF.Copy),
                  reads=[("ps", b0)], writes=[("qkf", "q")], extra=t2_guard)
            P.add("act", lambda e, b1=b1: e.activation(out=qkf[:, 512:640], in_=bank(b1)[:, 0:128], func=AF.Copy),
                  reads=[("ps", b1)], writes=[("qkf", "k")], extra=t2_guard)
            P.tag = 'red'
            ssq_tb = ssq[:, tb * 10:(tb + 1) * 10]
            rq_tb = rq[:, tb * 10:(tb + 1) * 10]
            P.add("dve", lambda e, ssq_tb=ssq_tb: e.tensor_reduce(out=ssq_tb, in_=sq.rearrange("p (h d) -> p h d", d=64), axis=AX.X, op=ALU.add),
                  reads=["sq"], writes=[("ssq", tb)])
            P.add("act", lambda e, ssq_tb=ssq_tb, rq_tb=rq_tb: e.activation(out=rq_tb, in_=ssq_tb, func=AF.Ln, scale=1.0 / 64, bias=EPS_AP),
                  reads=[("ssq", tb), "eps"], writes=[("rq", tb)])
            P.add("act", lambda e, rq_tb=rq_tb: e.activation(out=rq_tb, in_=rq_tb, func=AF.Exp, scale=-0.5),
                  reads=[("rq", tb)], writes=[("rq", tb)])
            P.tag = 't1'
            P.add("dve", lambda e, b0=b0, tb=tb: e.tensor_tensor(out=t1[:, 0:512].rearrange("p (h d) -> p h d", d=64),
                                                                 in0=qkf[:, 0:512].rearrange("p (h d) -> p h d", d=64),
                                                                 in1=CgQ[:, tb, :].unsqueeze(1).broadcast_to([128, 8, 64]), op=ALU.mult),
                  reads=[("qkf", "q"), "CgQ"], writes=["t1q"], extra=t2_guard)
            P.add("dve", lambda e, b1=b1, tb=tb: e.tensor_tensor(out=t1[:, 512:640].rearrange("p (h d) -> p h d", d=64),
                                                                 in0=qkf[:, 512:640].rearrange("p (h d) -> p h d", d=64),
                                                                 in1=CgK[:, tb, :].unsqueeze(1).broadcast_to([128, 2, 64]), op=ALU.mult),
                  reads=[("qkf", "k"), "CgK"], writes=["t1k"], extra=t2_guard)
            P.tag = 't2'
            for half, (so, do) in enumerate(((16, 0), (0, 16))):
                P.add("dve", lambda e, b0=b0, tb=tb, so=so, do=do: e.tensor_tensor(
                    out=t2[:, 0:512].rearrange("p (h g w) -> p h g w", g=2, w=32)[:, :, :, do:do + 16],
                    in0=qkf[:, 0:512].rearrange("p (h g w) -> p h g w", g=2, w=32)[:, :, :, so:so + 16],
                    in1=SgQ[:, tb, :].rearrange("p (g w) -> p g w", w=32)[:, :, do:do + 16].unsqueeze(1).broadcast_to([128, 8, 2, 16]),
                    op=ALU.mult),
                    reads=[("qkf", "q"), "SgQ"], writes=[("sq", "t2q", half)], extra=t2_guard)
                P.add("dve", lambda e, b1=b1, tb=tb, so=so, do=do: e.tensor_tensor(
                    out=t2[:, 512:640].rearrange("p (h g w) -> p h g w", g=2, w=32)[:, :, :, do:do + 16],
                    in0=qkf[:, 512:640].rearrange("p (h g w) -> p h g w", g=2, w=32)[:, :, :, so:so + 16],
                    in1=SgK[:, tb, :].rearrange("p (g w) -> p g w", w=32)[:, :, do:do + 16].unsqueeze(1).broadcast_to([128, 2, 2, 16]),
                    op=ALU.mult),
                    reads=[("qkf", "k"), "SgK"], writes=[("sq", "t2k", half)], extra=t2_guard)
            P.tag = 'add'
            P.add("dve", lambda e: e.tensor_tensor(out=t1, in0=t1, in1=t2, op=ALU.add),
                  reads=["t1q", "t1k", "sq"], writes=["t1q", "t1k"])
            qi = tb % 2
            qkt = qk_tok[qi]
            P.tag = 'qkt'
            P.add("dve", lambda e, qkt=qkt, rq_tb=rq_tb: e.tensor_tensor(
                out=qkt[:, 0:512].rearrange("p (h d) -> p h d", d=64), in0=t1[:, 0:512].rearrange("p (h d) -> p h d", d=64),
                in1=rq_tb[:, 0:8].unsqueeze(2).broadcast_to([128, 8, 64]), op=ALU.mult),
                reads=["t1q", ("rq", tb)], writes=[("qkt%d" % qi, "q")])
            P.tag = 'dup'
            for kv in range(2):
                P.add("dve", lambda e, qkt=qkt, rq_tb=rq_tb, kv=kv: e.tensor_tensor(
                    out=qkt[:, 512 + kv * 128:512 + (kv + 1) * 128].rearrange("p (r d) -> p r d", d=64),
                    in0=t1[:, 512 + kv * 64:512 + (kv + 1) * 64].unsqueeze(1).broadcast_to([128, 2, 64]),
                    in1=rq_tb[:, 8 + kv:9 + kv].unsqueeze(2).broadcast_to([128, 2, 64]), op=ALU.mult),
                    reads=["t1k", ("rq", tb)], writes=[("qkt%d" % qi, "k", kv)])

        def tok_tr(tb):
            tsl = slice(tb * 128, (tb + 1) * 128)
            b0, b1 = (0, 1) if tb % 2 == 0 else (2, 3)
            qi = tb % 2
            qkt = qk_tok[qi]
            P.tag = 'tr'
            tbank = TB1
            tps = bank_bf(tbank).rearrange("p (k t) -> p k t", k=8)
            for c6 in range(6):
                P.add("pe", lambda e, c6=c6, qkt=qkt, tps=tps: e.transpose(tps[:, c6, :], qkt[:, c6 * 128:(c6 + 1) * 128], ident_bf),
                      reads=["qkt%d" % qi, "ident"], writes=[("ps", tbank)])
            P.add("act", lambda e, tps=tps, tsl=tsl: e.activation(out=QKA[:, :, tsl], in_=tps[:, 0:6, :], func=AF.Copy),
                  reads=[("ps", tbank)], writes=[("QKA", tb)])
            P.tag = None

        MERGE = KNOBS.get('merge', False)
        n_tb = KNOBS.get('p2_tb', NT)
        tok_mm(0)
        for tb in range(n_tb):
            if tb + 1 < n_tb:
                tok_mm(tb + 1)
            if tb < 8:
                if tb + 1 < 8:
                    btab_dma(tb + 1)
                btab_compute(tb)
            tok_post(tb)
            njobs = 1 if MERGE else (4 if tb < 4 else 3)
            for _ in range(njobs if KNOBS.get('p2_feat', True) else 0):
                emit_feat_job()
            tok_tr(tb)
            if tb == 1:
                gate_part()
            if tb == 3:
                gate_part_b()
        while feat_state["i"] < len(feat_jobs) and KNOBS.get('p2_feat', True) and (not MERGE or stop == 2):
            emit_feat_job()
        ph2_last_pe = P.last["pe"]
        ph2_last_dve = P.last["dve"]
        ph2_last_act = P.last["act"]


        if stop == 2:
            dump("QKA", QKA, [128, 6, 2048], BF16)
            dump("QB", QB, [128, 4, 2048], BF16)
            dump("KB", KB, [128, 2, 2048], BF16)
            dump("VA", bf16v(o_VA, 1024), [128, 2048], BF16)
            dump("VB", bf16v(o_VB, 1024), [128, 2048], BF16)
            dump("GT", GT, [128, 8, 2048], BF16)
            finish()
            return nc
        o_A = (o_T + 1024) if MERGE else o_hT
        PT = [bf16v(o_A + i * 512, 512).rearrange("p (u q) -> p u q", u=2) for i in range(3)]
        Rt = f32v(o_A + 1536, 512)
        Tt = f32v(o_A + 2048, 512)
        OBs = [f32v(o_A + 2560 + i * 512, 512) for i in range(2)]
        wout_v = wout_d.rearrange("(kc p) n -> p kc n", p=128)
        if not MERGE:
            P.dma("pool", "wout", Wout, wout_v, writes=["Wout"], extra=[ph2_last_pe])
        itersA = [(hp, qc, kb) for hp in range(4) for qc in range(4) for kb in range(NT)]

        def a_qk(i):
            hp, qc, kb = itersA[i]
            kv = hp // 2
            qsl = slice(qc * 512, (qc + 1) * 512)
            ksl = slice(kb * 128, (kb + 1) * 128)
            sL, sU = (0, 1) if i % 2 == 0 else (2, 3)
            rd = [("QKA", kb)] + [("QKA", t) for t in range(qc * 4, qc * 4 + 4)]
            P.add("pe", lambda e: e.matmul(bank(sL), QKA[0:64, 4 + kv, ksl], QKA[0:64, hp, qsl], start=True, stop=True),
                  reads=rd, writes=[("ps", sL)])
            P.add("pe", lambda e: e.matmul(bank(sU), QKA[64:128, 4 + kv, ksl], QKA[64:128, hp, qsl], start=True, stop=True),
                  reads=rd, writes=[("ps", sU)])

        def a_exp(i):
            sL, sU = (0, 1) if i % 2 == 0 else (2, 3)
            pi = i % 3
            P.add("act", lambda e: e.activation(out=PT[pi], in_=psum[:, sL:sL + 2, :], func=AF.Exp),
                  reads=[("ps", sL), ("ps", sU)], writes=[("PT", pi)])

        def a_pv(i):
            hp, qc, kb = itersA[i]
            kv = hp // 2
            g = i // NT
            OB1, OB2 = (4, 5) if (g % 2 == 0 or MERGE) else (6, 7)
            pi = i % 3
            st, sp_ = (kb == 0), (kb == NT - 1)
            P.add("pe", lambda e: e.matmul(bank(OB1, slice(0, 64)), VA[:, kb, kv, :], PT[pi][:, 0, :], start=st, stop=sp_),
                  reads=[("VA", kb), ("PT", pi)], writes=[("ps", OB1)])
            P.add("pe", lambda e: e.matmul(bank(OB1, slice(64, 128)), VA[:, kb, kv, :], PT[pi][:, 1, :], start=st, stop=sp_),
                  reads=[("VA", kb), ("PT", pi)], writes=[("ps", OB1)])
            P.add("pe", lambda e: e.matmul(bank(OB2, slice(0, 64)), ones_bf, PT[pi][:, 0, :], start=st, stop=sp_),
                  reads=["ones_bf", ("PT", pi)], writes=[("ps", OB2)])
            P.add("pe", lambda e: e.matmul(bank(OB2, slice(64, 128)), ones_bf, PT[pi][:, 1, :], start=st, stop=sp_),
                  reads=["ones_bf", ("PT", pi)], writes=[("ps", OB2)])

        def a_post(i):
            hp, qc, kb = itersA[i]
            g = i // NT
            OB1, OB2 = (4, 5) if (g % 2 == 0 or MERGE) else (6, 7)
            qsl = slice(qc * 512, (qc + 1) * 512)
            if MERGE:
                P.add("dve", lambda e: e.tensor_copy(out=OBs[0], in_=bank(OB1)), reads=[("ps", OB1)], writes=["OBs0"])
                P.add("dve", lambda e: e.tensor_copy(out=OBs[1], in_=bank(OB2)), reads=[("ps", OB2)], writes=["OBs1"])
                o_src, d_src = OBs[0], OBs[1]
                r1, r2 = ["OBs0"], ["OBs1"]
            else:
                o_src, d_src = bank(OB1), bank(OB2)
                r1, r2 = [("ps", OB1)], [("ps", OB2)]
            P.add("dve", lambda e: e.reciprocal(out=Rt, in_=d_src), reads=r2, writes=["RtL", "RtU"])
            P.add("dve", lambda e: e.tensor_tensor(out=Tt, in0=Rt, in1=GT[:, hp, qsl], op=ALU.mult),
                  reads=["RtL", "RtU"] + [("GT", hp, qc)], writes=["Tt"])
            P.add("dve", lambda e: e.tensor_tensor(out=GT[:, hp, qsl], in0=o_src, in1=Tt, op=ALU.mult),
                  reads=r1 + ["Tt"], writes=[("OG", hp, qc, 0), ("OG", hp, qc, 1)])

        nA = len(itersA) if KNOBS.get("p3", True) else 0
        if nA:
            a_qk(0)
            a_qk(1)
        for i in range(nA):
            a_exp(i)
            if i + 2 < nA:
                a_qk(i + 2)
            a_pv(i)
            if itersA[i][2] == NT - 1:
                a_post(i)
            if MERGE and i % 6 == 5:
                emit_feat_job(banks=(6, 7))
        if MERGE:
            while feat_state["i"] < len(feat_jobs):
                emit_feat_job(banks=(6, 7))
            P.dma("pool", "wout", Wout, wout_v, writes=["Wout"], extra=[P.last["pe"]])

        if stop == 3:
            dump("GT", GT, [128, 8, 2048], BF16)
            finish()
            return nc
        eskrow = big[0:1, o_E:o_E + 512].bitcast(BF16).rearrange("p (h q) -> p h q", h=8)
        P.add("dve", lambda e: e.tensor_copy(out=eskrow, in_=esk8[0:1, :].unsqueeze(2).broadcast_to([1, 8, 128])),
              reads=["esk8"], writes=["eskrow"], extra=[P.last["dve"], P.last["act"], P.last["pe"]])
        o_B = o_hT + 2560
        etmpB = [f32v(o_B + i * 1024, 1024).rearrange("p (u c) -> p u c", u=2) for i in range(2)]
        PTB = [bf16v(o_B + 2048 + i * 512, 512).rearrange("p (u c) -> p u c", u=2) for i in range(6)]
        RtB = Rt
        TtB = Tt

        def v4(ap):
            return ap.rearrange("p (b h q) -> p b h q", b=2, h=2)

        def bsl(ap, blk):
            return ap[:, blk * 256:(blk + 1) * 256]

        groupsB = ([(kv, np_) for np_ in range(NT // 2) for kv in range(2)] if KNOBS.get("overlap5", False)
                   else [(kv, np_) for kv in range(2) for np_ in range(NT // 2)])
        itersB = [(g, c) for g in range(len(groupsB)) for c in range(3)]

        def b_valid(n, c):
            return 0 <= n - 1 + c < NT

        def b_qk(i):
            g, c = itersB[i]
            kv, np_ = groupsB[g]
            sL, sU = (0, 1) if (i % 2 == 0 or KNOBS.get("overlap5", False)) else (2, 3)
            for blk in range(2):
                n = 2 * np_ + blk
                if not b_valid(n, c):
                    continue
                kb = n - 1 + c
                nsl = slice(n * 128, (n + 1) * 128)
                ksl = slice(kb * 128, (kb + 1) * 128)
                rd = [("KB", kv, kb // 4)] + [("QB", 2 * kv + j, n // 4) for j in range(2)]
                P.add("pe", lambda e, blk=blk, ksl=ksl, nsl=nsl: e.matmul(bsl(bank(sL), blk), KB[0:64, kv, ksl],
                                                                          QB[0:64, 2 * kv:2 * kv + 2, nsl], start=True, stop=True),
                      reads=rd, writes=[("ps", sL)])
                P.add("pe", lambda e, blk=blk, ksl=ksl, nsl=nsl: e.matmul(bsl(bank(sU), blk), KB[64:128, kv, ksl],
                                                                          QB[64:128, 2 * kv:2 * kv + 2, nsl], start=True, stop=True),
                      reads=rd, writes=[("ps", sU)])

        def b_exp(i):
            g, c = itersB[i]
            kv, np_ = groupsB[g]
            sL, sU = (0, 1) if (i % 2 == 0 or KNOBS.get("overlap5", False)) else (2, 3)
            ei = i % 2
            slot = (g % 2) * 3 + c
            P.add("act", lambda e: e.activation(out=etmpB[ei], in_=psum[:, sL:sL + 2, :], func=AF.Exp, scale=0.125),
                  reads=[("ps", sL), ("ps", sU)], writes=[("etmpB", ei)])
            for u in range(2):
                et_ap = ET[:, 4 * kv:4 * kv + 4, c, :].rearrange("p (hh u) q -> p u hh q", u=2)[:, u, :, :].unsqueeze(1).broadcast_to([128, 2, 2, 128])
                eng = KNOBS.get("b_mul_eng", "pool") if (u == 0 and c != 2) else "dve"
                P.add(eng, lambda e, u=u, et_ap=et_ap: e.tensor_tensor(out=v4(PTB[slot][:, u, :]), in0=v4(etmpB[ei][:, u, :]), in1=et_ap, op=ALU.mult),
                      reads=[("etmpB", ei)] + [("ET", 4 * kv + j) for j in range(4)], writes=[("PTB", slot, u)])

        def b_pv(g, part):
            kv, np_ = groupsB[g]
            OB1, OB2 = (4, 5) if g % 2 == 0 else (6, 7)
            for blk in range(2):
                n = 2 * np_ + blk
                cs = [c for c in range(3) if b_valid(n, c)]
                if blk == 0:
                    todo = [c for c in cs if (c <= 1) == (part == 0)]
                    open_groups = (part == 0)
                else:
                    todo = cs if part == 1 else []
                    open_groups = (part == 1)
                esk_u = eskrow[:, 4 * kv:4 * kv + 4, :].rearrange("p (hh u) q -> p u hh q", u=2)
                if open_groups:
                    P.add("pe", lambda e, blk=blk: e.matmul(bsl(bank(OB2, slice(64, 128)), blk), ones_bf[0:1, :], esk_u[:, 1, :, :],
                                                            start=True, stop=False),
                          reads=["ones_bf", "eskrow"], writes=[("ps", OB2)])
                    P.add("pe", lambda e, blk=blk: e.matmul(bsl(bank(OB2, slice(0, 64)), blk), ones_bf[0:1, :], esk_u[:, 0, :, :],
                                                            start=True, stop=False),
                          reads=["ones_bf", "eskrow"], writes=[("ps", OB2)])
                for c in todo:
                    kb = n - 1 + c
                    slot = (g % 2) * 3 + c
                    first, last = (c == cs[0]), (c == cs[-1])
                    ptl = bsl(PTB[slot][:, 0, :], blk)
                    ptu = bsl(PTB[slot][:, 1, :], blk)
                    rdp = [("PTB", slot, 0), ("PTB", slot, 1)]
                    P.add("pe", lambda e, blk=blk, kb=kb, ptl=ptl, first=first, last=last: e.matmul(
                        bsl(bank(OB1, slice(0, 64)), blk), VB[:, kb, kv, :], ptl, start=first, stop=last),
                        reads=[("VB", kb)] + rdp, writes=[("ps", OB1)])
                    P.add("pe", lambda e, blk=blk, kb=kb, ptu=ptu, first=first, last=last: e.matmul(
                        bsl(bank(OB1, slice(64, 128)), blk), VB[:, kb, kv, :], ptu, start=first, stop=last),
                        reads=[("VB", kb)] + rdp, writes=[("ps", OB1)])
                    P.add("pe", lambda e, blk=blk, ptl=ptl, first=first, last=last: e.matmul(
                        bsl(bank(OB2, slice(0, 64)), blk), ones_bf, ptl, start=False, stop=last),
                        reads=["ones_bf"] + rdp, writes=[("ps", OB2)])
                    P.add("pe", lambda e, blk=blk, ptu=ptu, first=first, last=last: e.matmul(
                        bsl(bank(OB2, slice(64, 128)), blk), ones_bf, ptu, start=False, stop=last),
                        reads=["ones_bf"] + rdp, writes=[("ps", OB2)])

        def b_post(g):
            kv, np_ = groupsB[g]
            OB1, OB2 = (4, 5) if g % 2 == 0 else (6, 7)
            tsl2 = slice(np_ * 256, (np_ + 1) * 256)
            P.add("act", lambda e: e.activation(out=RtB, in_=bank(OB2), func=AF.Ln), reads=[("ps", OB2)], writes=["RtL", "RtU"])
            P.add("act", lambda e: e.activation(out=RtB, in_=RtB, func=AF.Exp, scale=-1.0), reads=["RtL", "RtU"], writes=["RtL", "RtU"])

            def gview(parts):
                return GT[parts, 4 + 2 * kv:6 + 2 * kv, tsl2].rearrange("p h (b q) -> p b h q", b=2)
            gjobs = [("GT", 4 + 2 * kv + j, np_ // 2) for j in range(2)]
            P.add("dve", lambda e: e.tensor_tensor(out=v4(TtB), in0=v4(RtB), in1=gview(slice(0, 128)), op=ALU.mult),
                  reads=["RtL", "RtU"] + gjobs, writes=["Tt"])
            P.add("dve", lambda e: e.tensor_tensor(out=gview(slice(0, 128)), in0=v4(bank(OB1)), in1=v4(TtB), op=ALU.mult),
                  reads=[("ps", OB1), "Tt"], writes=[("OGB", kv, np_, 0), ("OGB", kv, np_, 1)])

        OVL = KNOBS.get("overlap5", False)
        ytile = [f32v(o_QKA + i * 1024, 1024) for i in range(3)]
        xr = [f32v(o_T + 1024 + i * 1024, 1024) for i in range(3)]
        junk5 = bf16v(o_T2, 512)
        p5_done = []

        def xr_load(tb):
            P.dma("sp", ("xr", tb % 3), xr[tb % 3], x_v[tb], writes=[("xr", tb % 3)], extra=[ph2_last_dve, ph2_last_act])

        def p5_block(tb, og_ready):
            if not p5_done:
                for t in range(3):
                    xr_load(t)
            p5_done.append(tb)
            tsl = slice(tb * 128, (tb + 1) * 128)
            if OVL:
                b0, b1 = 2, 3
            else:
                b0, b1 = (0, 1) if tb % 2 == 0 else (2, 3)
            xi = tb % 3
            for n, bb in ((0, b0), (1, b1)):
                for ch in range(8):
                    P.add("pe", lambda e, ch=ch, n=n, bb=bb: e.matmul(bank(bb), GT[:, ch, tsl], Wout[:, ch, n * 512:(n + 1) * 512],
                                                                      start=(ch == 0), stop=(ch == 7)),
                          reads=["Wout"], writes=[("ps", bb)], extra=og_ready)
            yps = psum[:, b0:b0 + 2, :]
            P.add("act", lambda e: e.activation(out=junk5.rearrange("p (u q) -> p u q", u=2), in_=yps, func=AF.Square,
                                                accum_out=ss5[:, tb:tb + 1]),
                  reads=[("ps", b0), ("ps", b1), "ss5"], writes=["junk5", ("ss5", tb)])
            P.add("act", lambda e: e.activation(out=rstd5[:, tb:tb + 1], in_=ss5[:, tb:tb + 1], func=AF.Ln, scale=1.0 / D, bias=EPS_AP),
                  reads=[("ss5", tb), "eps"], writes=[("rstd5", tb)])
            P.add("act", lambda e: e.activation(out=rstd5[:, tb:tb + 1], in_=rstd5[:, tb:tb + 1], func=AF.Exp, scale=-0.5),
                  reads=[("rstd5", tb)], writes=[("rstd5", tb)])
            yi = len(p5_done) % 3
            P.add("dve", lambda e: e.scalar_tensor_tensor(
                out=ytile[yi].rearrange("p (u q) -> p u q", u=2), in0=yps, scalar=rstd5[:, tb:tb + 1],
                in1=GG.rearrange("p (u q) -> p u q", u=2), op0=ALU.mult, op1=ALU.mult),
                reads=[("ps", b0), ("ps", b1), ("rstd5", tb), ("GG", 0), ("GG", 1)], writes=[("ytile", yi)])
            P.add("dve", lambda e: e.tensor_tensor(out=ytile[yi], in0=ytile[yi], in1=xr[xi], op=ALU.add),
                  reads=[("ytile", yi), ("xr", xi)], writes=[("ytile", yi)])
            P.dma("sp", ("out", yi), out_v[tb], ytile[yi], reads=[("ytile", yi)], writes=[("outd", tb)])
            if tb + 3 < NT:
                xr_load(tb + 3)

        nB = len(itersB)
        b_qk(0)
        if not OVL:
            b_qk(1)
        for i in range(nB):
            b_exp(i)
            if OVL:
                if i + 1 < nB:
                    b_qk(i + 1)
            elif i + 2 < nB:
                b_qk(i + 2)
            if itersB[i][1] == 1:
                b_pv(itersB[i][0], 0)
            if itersB[i][1] == 2:
                b_pv(itersB[i][0], 1)
                if itersB[i][0] >= 1:
                    gprev = itersB[i][0] - 1
                    b_post(gprev)
                    if OVL and KNOBS.get('p5_inloop', True) and groupsB[gprev][0] == 1:
                        rdy = [P.last["dve"]]
                        for t in (2 * groupsB[gprev][1], 2 * groupsB[gprev][1] + 1):
                            p5_block(t, rdy)
        b_post(len(groupsB) - 1)
        ph4_last_dve = P.last["dve"]
        ph4_last_act = P.last["act"]

        if stop == 4:
            dump("GT", GT, [128, 8, 2048], BF16)
            finish()
            return nc
        rdy_tail = [ph4_last_dve]
        for tb in range(NT):
            if tb not in p5_done:
                p5_block(tb, rdy_tail)

        finish()
    return nc


EPS_AP = EPS
KNOBS = {}


def _host_inputs(x, c, w_ada, b_ada, g_pre, g_post, w_in, qn_a, kn_a, sink_b, w_out, rel_table):
    f = np.float32
    x = np.asarray(x, f)
    c = np.asarray(c, f)
    w_ada = np.ascontiguousarray(np.asarray(w_ada, f)[0])
    b_ada = np.asarray(b_ada, f)[0]
    g_pre = np.asarray(g_pre, f)[0]
    g_post = np.asarray(g_post, f)[0]
    w_in = np.asarray(w_in, f)[0]
    qn = np.asarray(qn_a, f)[0]
    kn = np.asarray(kn_a, f)[0]
    sink = np.asarray(sink_b, f)[0]
    w_out = np.ascontiguousarray(np.asarray(w_out, f)[0])
    rel = np.ascontiguousarray(np.asarray(rel_table, f))
    qa, ka, va, ga = w_in[:, 0:512], w_in[:, 512:640], w_in[:, 640:768], w_in[:, 768:1280]
    qb, kb, vb, gb = w_in[:, 1280:1792], w_in[:, 1792:1920], w_in[:, 1920:2048], w_in[:, 2048:2560]
    w_in_r = np.ascontiguousarray(np.concatenate([qa, ka, va, vb, ga, qb, kb, gb], axis=1))
    assert w_in_r.shape == (1024, 2560)
    swap = np.array([i + 16 if (i % 32) < 16 else i - 16 for i in range(64)])
    gains = np.ascontiguousarray(np.concatenate([qn, qn[swap], kn, kn[swap]])[None, :])
    consts = _constants()
    shared = dict(
        w_ada=w_ada,
        b_adaT=np.ascontiguousarray(b_ada.reshape(24, 128).T),
        b_gate=np.ascontiguousarray(b_ada[2048:3072][None, :]),
        g_preT=np.ascontiguousarray(g_pre.reshape(8, 128).T),
        g_post=np.ascontiguousarray(g_post[None, :]),
        w_in=w_in_r,
        gains=gains,
        sink=np.ascontiguousarray(sink[None, :]),
        rel_table=rel,
        w_out=w_out,
        **consts,
    )
    in_maps = []
    for b in range(N_CORES):
        m = dict(shared)
        m["x"] = np.ascontiguousarray(x[b])
        m["cT"] = np.ascontiguousarray(c[b].reshape(8, 128).T)
        in_maps.append(m)
    return in_maps


_NC_CACHE = {}


def kernel(x, c, w_ada, b_ada, g_pre, g_post, w_in, qn_a, kn_a, sink_b, w_out, rel_table):
    in_maps = _host_inputs(x, c, w_ada, b_ada, g_pre, g_post, w_in, qn_a, kn_a, sink_b, w_out, rel_table)
    if "nc" not in _NC_CACHE:
        _NC_CACHE["nc"] = build_program()
    nc = _NC_CACHE["nc"]
    res = run_bass_kernel_spmd(nc, in_maps, core_ids=list(range(N_CORES)))
    out = np.stack([np.asarray(r["out"], np.float32) for r in res.results], axis=0)
    return out
```
